# Optimizing a Trainium2 kernel written in Bass

```python
import math
import jax, jax.numpy as jnp
from jax import lax
import numpy as np

D_MODEL = 1024
BATCH = 4
SEQ = 4096
DEPTH = 2

S5_WIDTH = D_MODEL // 2
S5_GROUP = 16
S5_GROUPS = S5_WIDTH // S5_GROUP
S5_STATE = 64
FOX_HEAD_DIM = 64
FOX_HEADS = (D_MODEL - S5_WIDTH) // FOX_HEAD_DIM
FOX_WIDTH = FOX_HEADS * FOX_HEAD_DIM
Q_BLOCK = 128
EVEN_IN = S5_WIDTH + 3 * FOX_WIDTH + FOX_HEADS
EVEN_MIX = S5_WIDTH + FOX_WIDTH
POOL_WIDTH = D_MODEL // 2
POOL_WINDOWS = (2, 4, 8, 16)
POOL_GROUPS = len(POOL_WINDOWS)
POOL_GROUP_DIM = POOL_WIDTH // POOL_GROUPS
SGU_WIDTH = D_MODEL // 2
SGU_GROUPS = 4
SGU_GROUP_DIM = SGU_WIDTH // SGU_GROUPS
CHUNK = 128
ODD_IN = POOL_WIDTH + 2 * SGU_WIDTH
ODD_MIX = POOL_WIDTH + SGU_WIDTH
D_FF = 4 * D_MODEL
N_EVEN = (DEPTH + 1) // 2
N_ODD = DEPTH // 2
EPS = 1e-6

kernel_name = 'hybrid_s5_fox_pool_sgu'


def rms_norm(x, g):
    xf = x.astype(jnp.float32)
    y = xf * lax.rsqrt(jnp.mean(xf * xf, axis=-1, keepdims=True) + EPS)
    return (y * g.astype(jnp.float32)).astype(x.dtype)


def layer_norm(x, g, b):
    xf = x.astype(jnp.float32)
    mu = jnp.mean(xf, axis=-1, keepdims=True)
    xc = xf - mu
    y = xc * lax.rsqrt(jnp.mean(xc * xc, axis=-1, keepdims=True) + EPS)
    return (y * g.astype(jnp.float32) + b.astype(jnp.float32)).astype(x.dtype)


def _complex_scan_combine(e_i, e_j):
    ar_i, ai_i, br_i, bi_i = e_i
    ar_j, ai_j, br_j, bi_j = e_j
    ar = ar_j * ar_i - ai_j * ai_i
    ai = ar_j * ai_i + ai_j * ar_i
    br = ar_j * br_i - ai_j * bi_i + br_j
    bi = ar_j * bi_i + ai_j * br_i + bi_j
    return (ar, ai, br, bi)


def s5_mixer(u, lam_re, lam_im, log_dt, b_re, b_im, c_re, c_im, d, w_glu):
    f32 = jnp.float32
    bsz, L, _ = u.shape
    uf = u.astype(f32)
    dt = jnp.exp(log_dt.astype(f32))[:, None]
    lr = lam_re.astype(f32)
    li = lam_im.astype(f32)
    mag = jnp.exp(lr * dt)
    ab_re = mag * jnp.cos(li * dt)
    ab_im = mag * jnp.sin(li * dt)
    den = lr * lr + li * li
    nr = ab_re - 1.0
    ni = ab_im
    q_re = (nr * lr + ni * li) / den
    q_im = (ni * lr - nr * li) / den
    br = b_re.astype(f32)
    bi = b_im.astype(f32)
    bb_re = q_re[..., None] * br - q_im[..., None] * bi
    bb_im = q_re[..., None] * bi + q_im[..., None] * br
    ut = jnp.swapaxes(uf.reshape(bsz, L, S5_GROUPS, S5_GROUP), 0, 1)
    bu_re = jnp.einsum('lbgh,gph->lbgp', ut, bb_re)
    bu_im = jnp.einsum('lbgh,gph->lbgp', ut, bb_im)
    a_re = jnp.broadcast_to(ab_re[None, None], (L, 1, S5_GROUPS, S5_STATE))
    a_im = jnp.broadcast_to(ab_im[None, None], (L, 1, S5_GROUPS, S5_STATE))
    _, _, x_re, x_im = lax.associative_scan(_complex_scan_combine, (a_re, a_im, bu_re, bu_im), axis=0)
    y = (jnp.einsum('lbgp,ghp->lbgh', x_re, c_re.astype(f32))
         - jnp.einsum('lbgp,ghp->lbgh', x_im, c_im.astype(f32)))
    y = jnp.swapaxes(y, 0, 1).reshape(bsz, L, S5_WIDTH) + d.astype(f32) * uf
    y = jax.nn.gelu(y)
    y = y * jax.nn.sigmoid(y @ w_glu.astype(f32))
    return y.astype(u.dtype)


def fox_attention(q, k, v, f_logit, b_f):
    f32 = jnp.float32
    bsz, L, H, Dh = q.shape
    log_f = jax.nn.log_sigmoid(f_logit.astype(f32) + b_f.astype(f32))
    F = jnp.cumsum(log_f, axis=1)
    F_k = jnp.transpose(F, (0, 2, 1))
    n_blk = L // Q_BLOCK
    q_blocks = jnp.swapaxes(q.reshape(bsz, n_blk, Q_BLOCK, H, Dh), 0, 1)
    F_blocks = jnp.swapaxes(F_k.reshape(bsz, H, n_blk, Q_BLOCK), 0, 2).swapaxes(1, 2)
    k_pos = jnp.arange(L)
    scale = Dh ** -0.5

    def block(args):
        i, qi, Fi = args
        s = jnp.einsum('bqhd,bkhd->bhqk', qi, k).astype(f32) * scale
        s = s + (Fi[..., None] - F_k[:, :, None, :])
        q_pos = i * Q_BLOCK + jnp.arange(Q_BLOCK)
        mask = k_pos[None, :] <= q_pos[:, None]
        s = jnp.where(mask[None, None], s, -jnp.inf)
        p = jax.nn.softmax(s, axis=-1).astype(v.dtype)
        return jnp.einsum('bhqk,bkhd->bqhd', p, v)

    out = lax.map(block, (jnp.arange(n_blk), q_blocks, F_blocks))
    return jnp.swapaxes(out, 0, 1).reshape(bsz, L, H * Dh)


def pool_mixer(xc, pool_w, pool_scale):
    f32 = jnp.float32
    bsz, L, _ = xc.shape
    xg = xc.astype(f32).reshape(bsz, L, POOL_GROUPS, POOL_GROUP_DIM)
    csum = jnp.cumsum(xg, axis=1)
    t = jnp.arange(L, dtype=f32)
    outs = []
    for g, w in enumerate(POOL_WINDOWS):
        cg = csum[:, :, g]
        lagged = jnp.pad(cg, ((0, 0), (w, 0), (0, 0)))[:, :L]
        cnt = jnp.minimum(t + 1.0, float(w))[None, :, None]
        outs.append((cg - lagged) / cnt - xg[:, :, g])
    pooled = jnp.stack(outs, axis=2)
    y = jnp.einsum('blgc,gcd->blgd', pooled, pool_w.astype(f32)).reshape(bsz, L, POOL_WIDTH)
    return (y * pool_scale.astype(f32)).astype(xc.dtype)


def sgu_mixer(u, v, ln_g, ln_b, w_s, b_s):
    bsz, L, _ = u.shape
    u = jax.nn.gelu(u)
    v = layer_norm(jax.nn.gelu(v), ln_g, ln_b)
    n_chunk = L // CHUNK
    vg = v.reshape(bsz, n_chunk, CHUNK, SGU_GROUPS, SGU_GROUP_DIM)
    causal = jnp.tril(jnp.ones((CHUNK, CHUNK), dtype=bool))
    ws = jnp.where(causal[None], w_s, jnp.zeros_like(w_s))
    mixed = jnp.einsum('gts,bnsgc->bntgc', ws, vg) + jnp.transpose(b_s)[None, None, :, :, None]
    return u * mixed.reshape(bsz, L, SGU_WIDTH)


def even_mixer(h, w_in, lam_re, lam_im, log_dt, b_re, b_im, c_re, c_im, d, w_glu, b_f, w_out):
    bsz, L, _ = h.shape
    z = h @ w_in
    s1 = S5_WIDTH
    s2 = s1 + FOX_WIDTH
    s3 = s2 + FOX_WIDTH
    s4 = s3 + FOX_WIDTH
    u, q, k, v, fl = z[..., :s1], z[..., s1:s2], z[..., s2:s3], z[..., s3:s4], z[..., s4:]
    y_a = s5_mixer(u, lam_re, lam_im, log_dt, b_re, b_im, c_re, c_im, d, w_glu)
    shp = (bsz, L, FOX_HEADS, FOX_HEAD_DIM)
    y_b = fox_attention(q.reshape(shp), k.reshape(shp), v.reshape(shp), fl, b_f)
    return jnp.concatenate([y_a, y_b], axis=-1) @ w_out


def odd_mixer(h, w_in, pool_w, pool_scale, ln_g, ln_b, w_s, b_s, w_out):
    z = h @ w_in
    s1 = POOL_WIDTH
    s2 = s1 + SGU_WIDTH
    xc, u, v = z[..., :s1], z[..., s1:s2], z[..., s2:]
    y_c = pool_mixer(xc, pool_w, pool_scale)
    y_d = sgu_mixer(u, v, ln_g, ln_b, w_s, b_s)
    return jnp.concatenate([y_c, y_d], axis=-1) @ w_out


def sq_relu_mlp(h, w1, w2):
    return jnp.square(jax.nn.relu(h @ w1)) @ w2


def setup_inputs(seed: int = 0) -> dict:
    key = jax.random.key(seed)
    ks = jax.random.split(key, 32)
    f32 = jnp.float32

    def nrm(k, shape, scale):
        return scale * jax.random.normal(k, shape, f32)

    G, P, H = S5_GROUPS, S5_STATE, S5_GROUP
    return {
        'x': nrm(ks[0], (BATCH, SEQ, D_MODEL), 1.0),
        'mix_pre_g': 1.0 + nrm(ks[1], (DEPTH, D_MODEL), 0.02),
        'mix_post_g': 1.0 + nrm(ks[2], (DEPTH, D_MODEL), 0.02),
        'mlp_pre_g': 1.0 + nrm(ks[3], (DEPTH, D_MODEL), 0.02),
        'mlp_post_g': 1.0 + nrm(ks[4], (DEPTH, D_MODEL), 0.02),
        'w_in_even': nrm(ks[5], (N_EVEN, D_MODEL, EVEN_IN), D_MODEL ** -0.5),
        's5_lam_re': -0.5 + nrm(ks[6], (N_EVEN, G, P), 0.01),
        's5_lam_im': jnp.pi * jnp.arange(P, dtype=f32) + nrm(ks[7], (N_EVEN, G, P), 0.01),
        's5_log_dt': jax.random.uniform(ks[8], (N_EVEN, G), f32, math.log(1e-3), math.log(1e-1)),
        's5_b_re': nrm(ks[9], (N_EVEN, G, P, H), (2 * H) ** -0.5),
        's5_b_im': nrm(ks[10], (N_EVEN, G, P, H), (2 * H) ** -0.5),
        's5_c_re': nrm(ks[11], (N_EVEN, G, H, P), P ** -0.5),
        's5_c_im': nrm(ks[12], (N_EVEN, G, H, P), P ** -0.5),
        's5_d': nrm(ks[13], (N_EVEN, S5_WIDTH), 1.0),
        's5_w_glu': nrm(ks[14], (N_EVEN, S5_WIDTH, S5_WIDTH), S5_WIDTH ** -0.5),
        'fox_b_f': jax.random.uniform(ks[15], (N_EVEN, FOX_HEADS), f32, 0.0, 3.0),
        'w_out_even': nrm(ks[16], (N_EVEN, EVEN_MIX, D_MODEL), EVEN_MIX ** -0.5),
        'w_in_odd': nrm(ks[17], (N_ODD, D_MODEL, ODD_IN), D_MODEL ** -0.5),
        'pool_w': nrm(ks[18], (N_ODD, POOL_GROUPS, POOL_GROUP_DIM, POOL_GROUP_DIM), POOL_GROUP_DIM ** -0.5),
        'pool_scale': 1.0 + nrm(ks[19], (N_ODD, POOL_WIDTH), 0.02),
        'sgu_ln_g': 1.0 + nrm(ks[20], (N_ODD, SGU_WIDTH), 0.02),
        'sgu_ln_b': nrm(ks[21], (N_ODD, SGU_WIDTH), 0.02),
        'sgu_w_s': nrm(ks[22], (N_ODD, SGU_GROUPS, CHUNK, CHUNK), CHUNK ** -0.5),
        'sgu_b_s': 1.0 + nrm(ks[23], (N_ODD, SGU_GROUPS, CHUNK), 0.1),
        'w_out_odd': nrm(ks[24], (N_ODD, ODD_MIX, D_MODEL), ODD_MIX ** -0.5),
        'mlp_w1': nrm(ks[25], (DEPTH, D_MODEL, D_FF), D_MODEL ** -0.5),
        'mlp_w2': nrm(ks[26], (DEPTH, D_FF, D_MODEL), D_FF ** -0.5),
    }


def reference(x, mix_pre_g, mix_post_g, mlp_pre_g, mlp_post_g,
              w_in_even, s5_lam_re, s5_lam_im, s5_log_dt, s5_b_re, s5_b_im, s5_c_re, s5_c_im,
              s5_d, s5_w_glu, fox_b_f, w_out_even,
              w_in_odd, pool_w, pool_scale, sgu_ln_g, sgu_ln_b, sgu_w_s, sgu_b_s, w_out_odd,
              mlp_w1, mlp_w2):
    for l in range(DEPTH):
        h = rms_norm(x, mix_pre_g[l])
        if l % 2 == 0:
            e = l // 2
            y = even_mixer(h, w_in_even[e], s5_lam_re[e], s5_lam_im[e], s5_log_dt[e],
                           s5_b_re[e], s5_b_im[e], s5_c_re[e], s5_c_im[e], s5_d[e], s5_w_glu[e],
                           fox_b_f[e], w_out_even[e])
        else:
            o = l // 2
            y = odd_mixer(h, w_in_odd[o], pool_w[o], pool_scale[o], sgu_ln_g[o], sgu_ln_b[o],
                          sgu_w_s[o], sgu_b_s[o], w_out_odd[o])
        x = x + rms_norm(y, mix_post_g[l])
        h = rms_norm(x, mlp_pre_g[l])
        x = x + rms_norm(sq_relu_mlp(h, mlp_w1[l], mlp_w2[l]), mlp_post_g[l])
    return x
```

```python
import contextlib
import math
import numpy as np
import concourse.bass as bass
import concourse.mybir as mybir
from concourse.bass_utils import run_bass_kernel_spmd

F32 = mybir.dt.float32
BF16 = mybir.dt.bfloat16
I32 = mybir.dt.int32
AF = mybir.ActivationFunctionType
ALU = mybir.AluOpType
AX = mybir.AxisListType

ENGS = ("tensor", "vector", "scalar", "gpsimd", "sync")
N_DSEM = 64

NT, T0, T1 = 32, 15, 16
N0 = NT - T0
N1 = NT - T1
D = 1024
EPS = 1e-6
TWO_PI = 2.0 * math.pi
MASKV = -240000.0


class Buf:
    __slots__ = ("name", "writers", "readers")

    def __init__(self, name=""):
        self.name = name
        self.writers = {}
        self.readers = {}


class Plan:
    def __init__(self, nc, sems, dsems):
        self.nc, self.sems, self.dsems = nc, sems, dsems
        self.ops = {e: [] for e in ENGS}
        self.cnt = {e: 0 for e in ENGS}
        self.seen = {e: {} for e in ENGS}
        self.needed = set()
        self.dval = [0] * N_DSEM
        self.dnext = 0
        self.dnx = {}
        self.flushed = {e: 0 for e in ENGS}
        self.marks = {e: [] for e in ENGS}
        self.markval = {}
        self.markcnt = {e: 0 for e in ENGS}
        self.last_real = {e: 0 for e in ENGS}

    def _need(self, eng, prod, val, waits):
        if prod == eng and eng == "tensor":
            return
        if not isinstance(prod, tuple) and val <= self.flushed[prod]:
            for m in self.marks[prod]:
                if m >= val:
                    val = m
                    break
        if self.seen[eng].get(prod, 0) >= val:
            return
        self.seen[eng][prod] = val
        waits.append((prod, val))
        if not isinstance(prod, tuple):
            self.needed.add((prod, val))

    def _deps(self, eng, me, reads, writes, waits):
        for b in reads:
            for p, v in b.writers.items():
                if p != me:
                    self._need(eng, p, v, waits)
                elif eng != "tensor" and not isinstance(me, tuple):
                    self._need(eng, p, v, waits)
        for b in writes:
            for p, v in list(b.writers.items()):
                if p == me:
                    continue
                if isinstance(p, tuple) and isinstance(me, tuple):
                    continue
                self._need(eng, p, v, waits)
            for p, v in b.readers.items():
                if p != me:
                    self._need(eng, p, v, waits)

    def _commit(self, me, val, reads, writes):
        for b in reads:
            b.readers[me] = val
        for b in writes:
            if isinstance(me, tuple):
                b.writers = {p: v for p, v in b.writers.items() if isinstance(p, tuple)}
            else:
                b.writers = {}
            b.writers[me] = val
            b.readers = {}

    def op(self, eng, fn, reads=(), writes=()):
        waits = []
        self._deps(eng, eng, reads, writes, waits)
        self.cnt[eng] += 1
        idx = self.cnt[eng]
        self.ops[eng].append([waits, fn, idx, None])
        self.last_real[eng] = idx
        self._commit(eng, idx, reads, writes)

    def dma(self, eng, fn, reads=(), writes=()):
        lo, hi = (0, 24) if eng == "sync" else (24, N_DSEM)
        k = self.dnx.get(eng, lo)
        self.dnx[eng] = lo + (k + 1 - lo) % (hi - lo)
        me = ("d", k)
        waits = []
        if self.dval[k] > 0:
            self._need(eng, me, self.dval[k], waits)
        self._deps(eng, me, reads, writes, waits)
        self.dval[k] += 16
        self.cnt[eng] += 1
        self.ops[eng].append([waits, fn, self.cnt[eng], k])
        self._commit(me, self.dval[k], reads, writes)

    def barrier(self):
        for e in ENGS:
            waits = []
            for p in ENGS:
                if p != e and self.last_real[p] > 0:
                    self._need(e, p, self.last_real[p], waits)
            for k in range(N_DSEM):
                if self.dval[k] > 0:
                    self._need(e, ("d", k), self.dval[k], waits)
            self.cnt[e] += 1
            self.ops[e].append([waits, None, self.cnt[e], None])

    def flush(self):
        nc, sems, dsems = self.nc, self.sems, self.dsems
        for e in ENGS:
            real = [o for o in self.ops[e] if o[1] is not None and o[3] is None]
            if real:
                self.needed.add((e, real[-1][2]))
            for (_, fn, idx, dk) in self.ops[e]:
                if (e, idx) in self.needed and fn is not None and dk is None:
                    self.markcnt[e] += 1
                    self.markval[(e, idx)] = self.markcnt[e]
                    self.marks[e].append(idx)
        plan = self

        def run(e, engine):
            for waits, fn, idx, dk in plan.ops[e]:
                for prod, val in waits:
                    if isinstance(prod, tuple):
                        engine.wait_ge(dsems[prod[1]], val)
                    else:
                        engine.wait_ge(sems[prod], plan.markval[(prod, val)])
                if fn is None:
                    continue
                ins = fn(engine)
                if dk is not None:
                    ins.then_inc(dsems[dk], 16)
                elif (e, idx) in plan.markval:
                    ins.then_inc(sems[e], 1)

        with nc.Block() as block:
            @block.tensor
            def _(eng):
                run("tensor", eng)

            @block.vector
            def _(eng):
                run("vector", eng)

            @block.scalar
            def _(eng):
                run("scalar", eng)

            @block.gpsimd
            def _(eng):
                run("gpsimd", eng)

            @block.sync
            def _(eng):
                run("sync", eng)
        for e in ENGS:
            self.flushed[e] = self.cnt[e]
            self.ops[e] = []


class K:
    def __init__(self, P):
        self.P = P
        self.rec = None

    def _do(self, fn):
        if self.rec is not None:
            self.rec.append(fn)
        else:
            fn()

    def record(self, f):
        assert self.rec is None
        self.rec = []
        try:
            f()
            return self.rec
        finally:
            self.rec = None

    def mm(self, out, lhsT, rhs, start, stop, R, W, **kw):
        self._do(lambda: self.P.op("tensor", lambda e: e.matmul(out, lhsT=lhsT, rhs=rhs, start=start, stop=stop, **kw), R, W))

    def tr(self, out, in_, ident, R, W):
        self._do(lambda: self.P.op("tensor", lambda e: e.transpose(out, in_, ident), R, W))

    def v(self, name, *a, R=(), W=(), eng="vector", **kw):
        self._do(lambda: self.P.op(eng, lambda e: getattr(e, name)(*a, **kw), R, W))

    def act(self, out, in_, func, R, W, **kw):
        self._do(lambda: self.P.op("scalar", lambda e: e.activation(out, in_, func, **kw), R, W))

    def dma(self, eng, out, in_, R, W, **kw):
        self._do(lambda: self.P.dma(eng, lambda e: e.dma_start(out=out, in_=in_, **kw), R, W))


def bcast(ap, axis, n):
    a = ap.unsqueeze(axis)
    shp = list(a.shape)
    shp[axis] = n
    return a.broadcast_to(shp)


def build_program(dbg=None):
    nc = bass.Bass("TRN2", target_bir_lowering=False)

    def din(name, shape):
        return nc.dram_tensor(name, list(shape), F32, kind="ExternalInput").ap()

    def dscr(name, shape, dt):
        return nc.dram_tensor(name, list(shape), dt, kind="Internal").ap()

    xin = din("xin", [NT * 128, D])
    kbias = din("kbias", [128, NT])
    icnt = din("icnt", [128, 4, 16])
    g_mixpre = din("mix_pre_g", [2, D]); g_mixpost = din("mix_post_g", [2, D])
    g_mlppre = din("mlp_pre_g", [2, D]); g_mlppost = din("mlp_post_g", [2, D])
    w_in_e = din("w_in_even", [1, D, 2056])
    lam_re = din("s5_lam_re", [1, 32, 64]); lam_im = din("s5_lam_im", [1, 32, 64])
    log_dt = din("s5_log_dt", [1, 32])
    b_re = din("s5_b_re", [1, 32, 64, 16]); b_im = din("s5_b_im", [1, 32, 64, 16])
    c_re = din("s5_c_re", [1, 32, 16, 64]); c_im = din("s5_c_im", [1, 32, 16, 64])
    s5_d = din("s5_d", [1, 512]); w_glu = din("s5_w_glu", [1, 512, 512])
    fox_bf = din("fox_b_f", [1, 8])
    w_out_e = din("w_out_even", [1, D, D])
    w_in_o = din("w_in_odd", [1, D, 1536])
    pool_w = din("pool_w", [1, 4, 128, 128]); pool_scale = din("pool_scale", [1, 512])
    ln_g = din("sgu_ln_g", [1, 512]); ln_b = din("sgu_ln_b", [1, 512])
    w_s = din("sgu_w_s", [1, 4, 128, 128]); b_s = din("sgu_b_s", [1, 4, 128])
    w_out_o = din("w_out_odd", [1, D, D])
    mlp_w1 = din("mlp_w1", [2, D, 4096]); mlp_w2 = din("mlp_w2", [2, 4096, D])
    out = nc.dram_tensor("out", [N1 * 128, D], F32, kind="ExternalOutput").ap()

    UT_d = dscr("UT_d", [4, 128, NT * 128], BF16)
    KT_d = dscr("KT_d", [8, 64, NT * 128], BF16)
    QT_d = dscr("QT_d", [8, 67, N0 * 128], BF16)
    V_d = dscr("V_d", [NT, 128, 544], BF16)
    YC_d = dscr("YC_d", [8, 128, N0 * 128], BF16)
    X1_d = dscr("X1_d", [N0 * 128, D], F32)
    X2_d = dscr("X2_d", [N0 * 128, D], F32)
    YC1_d = dscr("YC1_d", [8, 128, N1 * 128], BF16)
    X3_d = dscr("X3_d", [N1 * 128, D], F32)
    W1b = [dscr(f"W1b{l}", [D, 4096], BF16) for l in range(2)]
    W2b = [dscr(f"W2b{l}", [4096, D], BF16) for l in range(2)]
    WOb = [dscr(f"WOb{l}", [D, D], BF16) for l in range(2)]
    WIOb = dscr("WIOb", [D, 1536], BF16)
    B_W1b = [Buf(), Buf()]; B_W2b = [Buf(), Buf()]; B_WOb = [Buf(), Buf()]; B_WIOb = Buf()
    dbg_o = None
    if dbg is not None:
        dbg_o = nc.dram_tensor("dbg", list(dbg), F32, kind="ExternalOutput").ap()

    B_UT, B_KT, B_QT, B_V, B_YC, B_X1, B_X2, B_YC1, B_X3, B_OUT = (Buf(n) for n in
        ["UT", "KT", "QT", "V", "YC", "X1", "X2", "YC1", "X3", "OUT"])

    es0 = contextlib.ExitStack()
    with es0:
        sems = {e: es0.enter_context(nc.semaphore("s_" + e)) for e in ENGS}
        dsems = [es0.enter_context(nc.semaphore(f"d{i}")) for i in range(N_DSEM)]
        P = Plan(nc, sems, dsems)
        k = K(P)

        def sb(es, name, shape, dt):
            return es.enter_context(nc.sbuf_tensor(name, list(shape), dt))

        def pst(es, name, shape, dt=F32):
            return es.enter_context(nc.psum_tensor(name, list(shape), dt))

        identb = sb(es0, "identb", [128, 128], BF16); identf = sb(es0, "identf", [128, 128], F32)
        triu = sb(es0, "triu", [128, 128], F32); onesf = sb(es0, "onesf", [128, 128], F32)
        maskb = sb(es0, "maskb", [128, 128], BF16); tril = sb(es0, "tril", [128, 128], F32)
        maskE = sb(es0, "maskE", [128, 2], F32); maskP = sb(es0, "maskP", [128, 4], F32)
        Fkb = sb(es0, "Fkb", [128, NT, 8], F32)
        kb_sb = sb(es0, "kb_sb", [128, NT], F32)
        B_const = Buf("const"); B_Fkb = Buf("Fkb")
        with contextlib.ExitStack() as es:
            dI = sb(es, "dI", [128, 128], I32); dF = sb(es, "dF", [128, 128], F32)
            pI = sb(es, "pI", [128, 8], I32); pF = sb(es, "pF", [128, 8], F32); ge = sb(es, "ge", [128, 8], F32)
            cI = sb(es, "cI", [128, 8], I32); cF = sb(es, "cF", [128, 8], F32)
            Bt = Buf("t")
            k.v("iota", dI[:], [[1, 128]], base=0, channel_multiplier=-1, eng="gpsimd", W=[Bt])
            k.v("tensor_copy", dF[:], dI[:], R=[Bt], W=[Bt])
            k.v("tensor_single_scalar", identf[:], dF[:], 0.0, ALU.is_equal, R=[Bt], W=[B_const])
            k.v("tensor_copy", identb[:], identf[:], R=[B_const], W=[B_const])
            k.v("tensor_single_scalar", triu[:], dF[:], 0.0, ALU.is_ge, R=[Bt], W=[B_const])
            k.v("tensor_single_scalar", tril[:], dF[:], 0.0, ALU.is_le, R=[Bt], W=[B_const])
            k.v("tensor_scalar", maskb[:], dF[:], 0.0, MASKV, ALU.is_lt, ALU.mult, R=[Bt], W=[B_const])
            k.v("memset", onesf[:], 1.0, W=[B_const])
            k.v("iota", pI[:], [[0, 8]], base=0, channel_multiplier=1, eng="gpsimd", W=[Bt])
            k.v("iota", cI[:], [[32, 8]], base=0, channel_multiplier=0, eng="gpsimd", W=[Bt])
            k.v("tensor_copy", pF[:], pI[:], R=[Bt], W=[Bt])
            k.v("tensor_copy", cF[:], cI[:], R=[Bt], W=[Bt])
            k.v("tensor_tensor", ge[:], pF[:], cF[:], ALU.is_ge, R=[Bt], W=[Bt])
            k.v("tensor_tensor", maskP[:], ge[:, 0:4], ge[:, 1:5], ALU.subtract, R=[Bt], W=[B_const])
            k.v("tensor_copy", maskE[:, 1:2], ge[:, 2:3], R=[Bt], W=[B_const])
            k.v("tensor_tensor", maskE[:, 0:1], ge[:, 0:1], ge[:, 2:3], ALU.subtract, R=[Bt], W=[B_const])
            k.dma("sync", kb_sb[:], kbias, [], [B_const])
            P.barrier(); P.flush()

        def load_w_bf16(wt, wbuf, src, nkt, ncols, col0=0, defer=None, per=8192, cw=2048, srcbuf=None, queue="gpsimd"):
            for c0 in range(0, ncols, cw):
                c1 = min(ncols, c0 + cw)
                step = max(1, per // (c1 - c0))
                for k0 in range(0, nkt, step):
                    k1 = min(nkt, k0 + step)

                    def thunk(k0=k0, k1=k1, c0=c0, c1=c1):
                        k.dma(queue, wt[:, k0:k1, c0:c1],
                              src[k0 * 128:k1 * 128, col0 + c0:col0 + c1].rearrange("(k p) c -> p k c", p=128),
                              [srcbuf] if srcbuf is not None else [], [wbuf])
                    if defer is None:
                        thunk()
                    else:
                        defer.append(thunk)

        conv_pend = []

        def convert_w(dst, bdst, src, rows_per, ncols):
            nrows = src.shape[0]
            for c0 in range(0, ncols, 2048):
                c1 = min(ncols, c0 + 2048)
                for r0 in range(0, nrows, rows_per):
                    conv_pend.append(lambda r0=r0, c0=c0, c1=c1: k.dma(
                        "gpsimd", dst[r0:r0 + rows_per, c0:c1], src[r0:r0 + rows_per, c0:c1], [], [bdst]))

        def convert_all_weights():
            convert_w(WOb[0], B_WOb[0], w_out_e[0], 512, D)
            convert_w(W1b[0], B_W1b[0], mlp_w1[0], 256, 4096)
            convert_w(W2b[0], B_W2b[0], mlp_w2[0], 512, D)
            n0 = len(conv_pend)
            convert_w(WIOb, B_WIOb, w_in_o[0], 512, 1536)
            convert_w(WOb[1], B_WOb[1], w_out_o[0], 512, D)
            convert_w(W1b[1], B_W1b[1], mlp_w1[1], 256, 4096)
            convert_w(W2b[1], B_W2b[1], mlp_w2[1], 512, D)
            conv_pend1.extend(conv_pend[n0:])
            del conv_pend[n0:]

        conv_pend1 = []

        def load_bcast_row(t, tb, src_row, n):
            k.dma("sync", t[:, 0:n], bcast(src_row, 0, 128), [], [tb])

        def rms_rstd(ss_ap, rstd, tmp, Bss, Brs, n):
            k.v("tensor_scalar", tmp, ss_ap, 1.0 / n, EPS, ALU.mult, ALU.add, R=[Bss], W=[Brs])
            k.act(tmp, tmp, AF.Sqrt, [Brs], [Brs])
            k.v("reciprocal", rstd, tmp, R=[Brs], W=[Brs])

        def norm_in(xt, Bx, gt, Bg, hb, Bh, junk, Bj, st, Bst):
            k.act(hb[:], xt[:], AF.Square, [Bx], [Bh, Bst], accum_out=st[:, 0:1])
            rms_rstd(st[:, 0:1], st[:, 2:3], st[:, 1:2], Bst, Bst, D)
            k.v("scalar_tensor_tensor", hb[:], xt[:], st[:, 2:3], gt[:, 0:D], ALU.mult, ALU.mult,
                R=[Bx, Bst, Bg], W=[Bh])

        def to_fm(hb, Bh, tp, Btp, hT, BhT, col0, use_act):
            for kt in range(8):
                k.tr(tp[:, kt * 128:(kt + 1) * 128], hb[:, kt * 128:(kt + 1) * 128], identb[:], [Bh, B_const], [Btp])
            src = tp[:].rearrange("p (k t) -> p k t", k=8)
            if use_act:
                k.act(hT[:, :, col0:col0 + 128], src, AF.Copy, [Btp], [BhT])
            else:
                k.v("tensor_copy", hT[:, :, col0:col0 + 128], src, R=[Btp], W=[BhT])

        pn_ctr = [0]

        def post_norm_res(o_ps, Bo, gt, Bg, xt, Bx, res, Br, junk, Bj, st, Bst, add_eng=None):
            k.act(res[:], o_ps, AF.Square, [Bo], [Br, Bst], accum_out=st[:, 0:1])
            rms_rstd(st[:, 0:1], st[:, 2:3], st[:, 1:2], Bst, Bst, D)
            k.v("scalar_tensor_tensor", res[:], o_ps, st[:, 2:3], gt[:, 0:D], ALU.mult, ALU.mult,
                R=[Bo, Bst, Bg], W=[Br])
            pn_ctr[0] += 1
            k.v("tensor_tensor", res[:], res[:], xt[:], ALU.add, R=[Br, Bx], W=[Br],
                eng=(add_eng or ("gpsimd" if pn_ctr[0] % 2 == 0 else "vector")))

        def phase_outproj_mlp(tag, YCd, B_YCd, ntile, w_out_l, g_row, xsrc, B_xsrc, xmid, B_xmid,
                              xdst, B_xdst, w1_l, w2_l, g_pre_row, g_post_row, Bwo_s, Bw1_s, Bw2_s):
            GT = 4
            with contextlib.ExitStack() as eo:
                w1 = sb(eo, "w1" + tag, [128, 8, 4096], BF16); Bw1 = Buf()
                w2 = sb(eo, "w2" + tag, [128, 32, D], BF16); Bw2 = Buf()
                gp = sb(eo, "g1" + tag, [128, D], F32); gq = sb(eo, "g2" + tag, [128, D], F32); Bgm = Buf()
                with contextlib.ExitStack() as es:
                    wo = sb(es, "wo" + tag, [128, 8, D], BF16); Bwo = Buf()
                    gt = sb(es, "go" + tag, [128, D], F32); Bg = Buf()
                    load_w_bf16(wo, Bwo, w_out_l, 8, D, srcbuf=Bwo_s, queue="sync")
                    load_bcast_row(gt, Bg, g_row, D)
                    pend = []
                    load_w_bf16(w1, Bw1, w1_l, 8, 4096, defer=pend, per=2048, cw=2048, srcbuf=Bw1_s, queue="sync")
                    load_w_bf16(w2, Bw2, w2_l, 32, D, defer=pend, per=2048, cw=1024, srcbuf=Bw2_s, queue="sync")
                    NB = 4
                    NP = 3
                    yc = [sb(es, f"yc{tag}{i}", [128, 8, 128], BF16) for i in range(NB)]
                    xt = [sb(es, f"xo{tag}{i}", [128, D], F32) for i in range(NB)]
                    rs = [sb(es, f"ro{tag}{i}", [128, D], F32) for i in range(NB)]
                    jk = sb(es, "jo" + tag, [128, D], BF16); Bj = Buf()
                    st = [sb(es, f"so{tag}{i}", [128, 4], F32) for i in range(NB)]
                    ops = [pst(es, f"po{tag}{i}", [128, D]) for i in range(NP)]
                    Byc = [Buf() for _ in range(NB)]; Bxt = [Buf() for _ in range(NB)]; Brs = [Buf() for _ in range(NB)]
                    Bst = [Buf() for _ in range(NB)]; Bop = [Buf() for _ in range(NP)]

                    def ld(t):
                        i = t % NB
                        k.dma("sync", yc[i][:], YCd[:, :, t * 128:(t + 1) * 128].rearrange("f p t -> p f t"), [B_YCd], [Byc[i]])
                        k.dma("sync", xt[i][:], xsrc[t * 128:(t + 1) * 128, :], [B_xsrc], [Bxt[i]])

                    ld(0)
                    if ntile > 1:
                        ld(1)
                    for t in range(ntile):
                        i = t % NB
                        pi = t % NP
                        if t + 2 < ntile:
                            ld(t + 2)
                        for _ in range(2):
                            if pend:
                                pend.pop(0)()
                        for hf in range(2):
                            for ft in range(8):
                                k.mm(ops[pi][:, hf * 512:(hf + 1) * 512], yc[i][:, ft, :], wo[:, ft, hf * 512:(hf + 1) * 512],
                                     ft == 0, ft == 7, [Byc[i], Bwo], [Bop[pi]])
                        post_norm_res(ops[pi][:], Bop[pi], gt, Bg, xt[i], Bxt[i], rs[i], Brs[i], jk, Bj, st[i], Bst[i],
                                      add_eng="vector")
                        k.dma("sync", xmid[t * 128:(t + 1) * 128, :], rs[i][:], [Brs[i]], [B_xmid])
                    while pend:
                        pend.pop(0)()
                    load_bcast_row(gp, Bgm, g_pre_row, D)
                    load_bcast_row(gq, Bgm, g_post_row, D)
                    P.barrier(); P.flush()
                with contextlib.ExitStack() as es:
                    Bg = Bgm
                    NX = GT + 1
                    xt = [sb(es, f"xm{tag}{i}", [128, D], F32) for i in range(NX)]; Bxt = [Buf() for _ in range(NX)]
                    hb = sb(es, "hbm" + tag, [128, D], BF16); Bhb = Buf()
                    st = sb(es, "sm" + tag, [128, 4], F32); Bst = Buf()
                    st2 = sb(es, "sm2" + tag, [128, 4], F32); Bst2 = Buf()
                    hT = sb(es, "hTm" + tag, [128, 8, GT * 128], BF16); BhT = Buf()
                    A = sb(es, "Am" + tag, [128, 32, GT * 128], BF16); BA = [Buf() for _ in range(32)]
                    rl = [sb(es, f"rl{tag}{i}", [128, GT * 128], BF16) for i in range(2)]; Brl = [Buf(), Buf()]
                    rs1 = sb(es, f"rsm{tag}", [128, D], F32); rs = [rs1, rs1]; Brs1 = Buf(); Brs = [Brs1, Brs1]
                    tp = pst(es, "tpm" + tag, [128, 1024], BF16); Btp = Buf()
                    hp = [pst(es, f"hp{tag}{i}", [128, 512]) for i in range(3)]; Bhp = [Buf() for _ in range(3)]
                    op_ = [pst(es, f"opm{tag}{i}", [128, D]) for i in range(2)]; Bop = [Buf(), Buf()]
                    groups_ = [(g0, min(GT, ntile - g0)) for g0 in range(0, ntile, GT)]

                    def xload(t):
                        k.dma("sync", xt[t % NX][:], xmid[t * 128:(t + 1) * 128, :], [B_xmid], [Bxt[t % NX]])

                    def prep(t, j):
                        norm_in(xt[t % NX], Bxt[t % NX], gp, Bg, hb, Bhb, None, None, st, Bst)
                        to_fm(hb, Bhb, tp, Btp, hT, BhT, j * 128, j % 2 == 1)

                    for j in range(groups_[0][1]):
                        xload(j)
                    for j in range(groups_[0][1]):
                        prep(j, j)
                    for gi_, (g0, nt) in enumerate(groups_):
                        n = nt * 128
                        nxt = groups_[gi_ + 1] if gi_ + 1 < len(groups_) else None
                        if nxt is not None:
                            xload(nxt[0])
                        for f in range(32):
                            pi = f % 3
                            if tag == "0" and conv_pend1 and f % 4 == 1:
                                conv_pend1.pop(0)()
                            for kt in range(8):
                                k.mm(hp[pi][:, 0:n], w1[:, kt, f * 128:(f + 1) * 128], hT[:, kt, 0:n], kt == 0, kt == 7,
                                     [Bw1, BhT], [Bhp[pi]])
                            ri = f % 2
                            k.act(rl[ri][:, 0:n], hp[pi][:, 0:n], AF.Relu, [Bhp[pi]], [Brl[ri]])
                            k.v("tensor_tensor", A[:, f, 0:n], rl[ri][:, 0:n], rl[ri][:, 0:n], ALU.mult,
                                R=[Brl[ri]], W=[BA[f]], eng=("vector" if f % 2 == 0 else "gpsimd"))
                        for j in range(nt):
                            t = g0 + j
                            oi = t % 2
                            for hf in range(2):
                                for f in range(32):
                                    k.mm(op_[oi][:, hf * 512:(hf + 1) * 512], A[:, f, j * 128:(j + 1) * 128],
                                         w2[:, f, hf * 512:(hf + 1) * 512], f == 0, f == 31, [BA[f], Bw2], [Bop[oi]])
                            if nxt is not None and j < nxt[1]:
                                prep(nxt[0] + j, j)
                            post_norm_res(op_[oi][:], Bop[oi], gq, Bg, xt[t % NX], Bxt[t % NX], rs[oi], Brs[oi], None, None,
                                          st2, Bst2)
                            k.dma("sync", xdst[t * 128:(t + 1) * 128, :], rs[oi][:], [Brs[oi]], [B_xdst])
                            if nxt is not None and j + 1 < nxt[1]:
                                xload(nxt[0] + j + 1)
                    if tag == "0":
                        while conv_pend1:
                            conv_pend1.pop(0)()
                    P.barrier(); P.flush()

        WE = w_in_e[0]
        with contextlib.ExitStack() as es:
            GT = 4
            NG = NT // GT
            wi = sb(es, "wiA", [128, 8, 2056], BF16); Bwi = Buf()
            gp = sb(es, "gA", [128, D], F32); Bg = Buf()
            bfb = sb(es, "bfb", [128, 8], F32)
            load_bcast_row(gp, Bg, g_mixpre[0], D)
            load_bcast_row(bfb, Bg, fox_bf[0], 8)
            load_w_bf16(wi, Bwi, WE, 8, 2056, per=4096, cw=512)
            xt = [[sb(es, f"xA{i}_{j}", [128, D], F32) for j in range(GT)] for i in range(2)]
            Bxt = [[Buf() for j in range(GT)] for i in range(2)]
            hb = [sb(es, f"hbA{j}", [128, D], BF16) for j in range(GT)]; Bhb = [Buf() for _ in range(GT)]
            jk = sb(es, "jA", [128, D], BF16); Bj = Buf()
            st = sb(es, "sA", [128, 3, GT], F32); Bst = Buf()
            hT = [sb(es, f"hTA{i}", [128, 8, GT * 128], BF16) for i in range(2)]; BhT = [Buf(), Buf()]
            us = [sb(es, f"usA{i}", [128, 4, 512], BF16) for i in range(2)]; Bus = [Buf(), Buf()]
            ks = [sb(es, f"ksA{i}", [128, 4, 512], BF16) for i in range(2)]; Bks = [Buf(), Buf()]
            q2 = [sb(es, f"q2A{i}", [128, 4, 512], BF16) for i in range(2)]; Bq2 = [Buf(), Buf()]
            qs = [sb(es, f"qsA{i}", [67, 8, 512], BF16) for i in range(2)]; Bqs = [Buf(), Buf()]
            vs = [sb(es, f"vsA{i}", [128, 8, 68], BF16) for i in range(4)]; Bvs = [Buf() for _ in range(4)]
            carry = sb(es, "carry", [128, 8], F32); Bcar = Buf()
            fx = sb(es, "fxA", [128, 8, GT, 8], F32); Bfx = Buf()
            fsp = sb(es, "fsp", [128, GT, 8, 3], BF16); Bfsp = Buf()
            tp = [pst(es, f"tpA{i}", [128, 1024], BF16) for i in range(2)]; Btp = [Buf(), Buf()]
            pp = [pst(es, f"ppA{i}", [128, 512]) for i in range(4)]; Bpp = [Buf() for _ in range(4)]
            pf = pst(es, "pfA", [128, 512]); Bpf = Buf()
            pa = pst(es, "paA", [128, 512]); Bpa = Buf()
            k.v("memset", carry[:], 0.0, W=[Bcar])
            for i in range(4):
                k.v("memset", vs[i][:, :, 64:68], 1.0, W=[Bvs[i]])
            ctr = {"pp": 0, "tp": 0}

            def nextpp():
                i = ctr["pp"] % 4; ctr["pp"] += 1
                return pp[i], Bpp[i]

            def s1_load(g):
                for j in range(GT):
                    t = g * GT + j
                    k.dma("sync", xt[g % 2][j][:], xin[t * 128:(t + 1) * 128, :], [], [Bxt[g % 2][j]])

            def s1_norm(g):
                X, BX = xt[g % 2], Bxt[g % 2]
                for j in range(GT):
                    k.act(hb[j][:], X[j][:], AF.Square, [BX[j]], [Bhb[j], Bst], accum_out=st[:, 0, j:j + 1])
                rms_rstd(st[:, 0, :], st[:, 2, :], st[:, 1, :], Bst, Bst, D)
                for j in range(GT):
                    k.v("scalar_tensor_tensor", hb[j][:], X[j][:], st[:, 2, j:j + 1], gp[:, 0:D], ALU.mult, ALU.mult,
                        R=[BX[j], Bst, Bg], W=[Bhb[j]])

            def s1_tr(g):
                for j in range(GT):
                    ti = ctr["tp"] % 2; ctr["tp"] += 1
                    to_fm(hb[j], Bhb[j], tp[ti], Btp[ti], hT[g % 2], BhT[g % 2], j * 128, j % 2 == 1)

            def s2_uk(g):
                gi = g % 2
                H, BH = hT[gi], BhT[gi]
                for ft in range(4):
                    p_, Bp_ = nextpp()
                    for kt in range(8):
                        k.mm(p_[:, :], wi[:, kt, ft * 128:(ft + 1) * 128], H[:, kt, :], kt == 0, kt == 7, [Bwi, BH], [Bp_])
                    if ft % 2 == 0:
                        k.v("tensor_copy", us[gi][:, ft, :], p_[:, :], R=[Bp_], W=[Bus[gi]])
                    else:
                        k.act(us[gi][:, ft, :], p_[:, :], AF.Copy, [Bp_], [Bus[gi]])
                k.dma("gpsimd", UT_d[:, :, g * 512:(g + 1) * 512].rearrange("f p t -> p f t"), us[gi][:], [Bus[gi]], [B_UT])
                for hp in range(4):
                    p_, Bp_ = nextpp()
                    c0 = 1024 + hp * 128
                    for kt in range(8):
                        k.mm(p_[:, :], wi[:, kt, c0:c0 + 128], H[:, kt, :], kt == 0, kt == 7, [Bwi, BH], [Bp_])
                    if hp % 2 == 0:
                        k.v("tensor_copy", ks[gi][:, hp, :], p_[:, :], R=[Bp_], W=[Bks[gi]])
                    else:
                        k.act(ks[gi][:, hp, :], p_[:, :], AF.Copy, [Bp_], [Bks[gi]])
                k.dma("gpsimd", KT_d[:, :, g * 512:(g + 1) * 512].rearrange("(hp two) p t -> (two p) hp t", two=2),
                      ks[gi][:], [Bks[gi]], [B_KT])

            def s2_q(g):
                gi = g % 2
                H, BH = hT[gi], BhT[gi]
                for hp in range(4):
                    p_, Bp_ = nextpp()
                    c0 = 512 + hp * 128
                    for kt in range(8):
                        k.mm(p_[:, :], wi[:, kt, c0:c0 + 128], H[:, kt, :], kt == 0, kt == 7, [Bwi, BH], [Bp_])
                    if hp % 2 == 1:
                        k.v("tensor_copy", q2[gi][:, hp, :], p_[:, :], R=[Bp_], W=[Bq2[gi]])
                    else:
                        k.act(q2[gi][:, hp, :], p_[:, :], AF.Copy, [Bp_], [Bq2[gi]])
                j0 = max(0, T0 - g * GT)
                cq0, cq1 = (g * GT + j0 - T0) * 128, ((g + 1) * GT - T0) * 128
                for two in range(2):
                    k.dma("gpsimd", QT_d[two::2, 0:64, cq0:cq1].rearrange("hp p t -> p hp t"),
                          q2[gi][two * 64:(two + 1) * 64, :, j0 * 128:GT * 128], [Bq2[gi]], [B_QT])

            def s2_vf(g, need_q):
                gi = g % 2
                H, BH = hT[gi], BhT[gi]
                for j in range(GT):
                    t = g * GT + j
                    p_, Bp_ = nextpp()
                    for kt in range(8):
                        k.mm(p_[:, :], H[:, kt, j * 128:(j + 1) * 128], wi[:, kt, 1536:2048], kt == 0, kt == 7, [Bwi, BH], [Bp_])
                    if j % 2 == 0:
                        k.v("tensor_copy", vs[j][:, :, 0:64], p_[:, :].rearrange("p (h d) -> p h d", h=8), R=[Bp_], W=[Bvs[j]])
                    else:
                        k.act(vs[j][:, :, 0:64], p_[:, :].rearrange("p (h d) -> p h d", h=8), AF.Copy, [Bp_], [Bvs[j]])
                    k.dma("gpsimd", V_d[t], vs[j][:].rearrange("p h d -> p (h d)"), [Bvs[j]], [B_V])
                for j in range(GT):
                    for kt in range(8):
                        k.mm(pf[:, j * 8:(j + 1) * 8], H[:, kt, j * 128:(j + 1) * 128], wi[:, kt, 2048:2056], kt == 0, kt == 7,
                             [Bwi, BH], [Bpf])
                k.v("tensor_tensor", fx[:, 0, :, :], pf[:, 0:32].rearrange("p (j h) -> p j h", j=GT), bcast(bfb[:], 1, GT),
                    ALU.add, R=[Bpf, Bg], W=[Bfx])
                k.act(fx[:, 1, :, :], fx[:, 0, :, :], AF.Exp, [Bfx], [Bfx], scale=-1.0)
                k.act(fx[:, 2, :, :], fx[:, 1, :, :], AF.Ln, [Bfx], [Bfx], bias=1.0)
                sp = fx[:, 2, :, :].rearrange("p j h -> p (j h)")
                k.mm(pf[:, 32:64], triu[:], sp, True, True, [Bfx, B_const], [Bpf])
                k.mm(pf[:, 64:96], onesf[:], sp, True, True, [Bfx, B_const], [Bpf])
                for j in range(GT):
                    t = g * GT + j
                    k.v("tensor_tensor", fx[:, 3, j, :], pf[:, 32 + j * 8:40 + j * 8], carry[:], ALU.add, R=[Bpf, Bcar], W=[Bfx])
                    k.v("tensor_scalar", Fkb[:, t, :], fx[:, 3, j, :], kb_sb[:, t:t + 1], None, ALU.add, R=[Bfx, B_const], W=[B_Fkb])
                    k.v("tensor_tensor", carry[:], pf[:, 64 + j * 8:72 + j * 8], carry[:], ALU.add, R=[Bpf, Bcar], W=[Bcar])
                if need_q:
                    k.v("tensor_scalar", fx[:, 4, :, :], fx[:, 3, :, :], -8.0, None, ALU.mult, R=[Bfx], W=[Bfx])
                    k.v("tensor_copy", fsp[:, :, :, 0], fx[:, 4, :, :], R=[Bfx], W=[Bfsp])
                    k.v("tensor_tensor", fx[:, 5, :, :], fx[:, 4, :, :], fsp[:, :, :, 0], ALU.subtract, R=[Bfx, Bfsp], W=[Bfx])
                    k.v("tensor_copy", fsp[:, :, :, 1], fx[:, 5, :, :], R=[Bfx], W=[Bfsp])
                    k.v("tensor_tensor", fx[:, 6, :, :], fx[:, 5, :, :], fsp[:, :, :, 1], ALU.subtract, R=[Bfx, Bfsp], W=[Bfx])
                    k.v("tensor_copy", fsp[:, :, :, 2], fx[:, 6, :, :], R=[Bfx], W=[Bfsp])
                    for hh in range(2):
                        for j in range(GT):
                            for h4 in range(4):
                                h = hh * 4 + h4
                                k.mm(pa[64:67, h4 * 128:(h4 + 1) * 128], fsp[:, j, h, :], identb[:], True, True,
                                     [Bfsp, B_const], [Bpa], tile_position=(0, 64))
                            k.v("tensor_copy", qs[gi][64:67, hh * 4:hh * 4 + 4, j * 128:(j + 1) * 128],
                                pa[64:67, :].rearrange("p (h t) -> p h t", h=4), R=[Bpa], W=[Bqs[gi]])
                    j0 = max(0, T0 - g * GT)
                    k.dma("gpsimd", QT_d[:, 64:67, (g * GT + j0 - T0) * 128:((g + 1) * GT - T0) * 128].rearrange("h p t -> p h t"),
                          qs[gi][64:67, :, j0 * 128:GT * 128], [Bqs[gi]], [B_QT])

            s1_load(0); s1_norm(0); s1_tr(0)
            for g in range(NG):
                need_q = (g * GT + GT - 1) >= T0
                if g + 1 < NG:
                    s1_load(g + 1)
                s2_uk(g)
                if g + 1 < NG:
                    s1_norm(g + 1)
                if need_q:
                    s2_q(g)
                s2_vf(g, need_q)
                if g + 1 < NG:
                    s1_tr(g + 1)
            P.barrier(); P.flush()

        with contextlib.ExitStack() as es5:
            C5 = s5_alloc(es5, sb)
            with contextlib.ExitStack() as esc:
                gen = s5_const_gen(nc, P, k, sb, pst, locals(), esc, C5)
                phase_fox(nc, P, k, sb, pst, locals(), bg=gen)
                for _ in gen:
                    pass
                P.barrier(); P.flush()
            s5_runtime(nc, P, k, sb, pst, locals(), C5)

        phase_outproj_mlp("0", YC_d, B_YC, N0, WOb[0], g_mixpost[0], xin[T0 * 128:NT * 128, :], Buf(), X1_d, B_X1,
                          X2_d, B_X2, W1b[0], W2b[0], g_mlppre[0], g_mlppost[0], B_WOb[0], B_W1b[0], B_W2b[0])

        phase_odd(nc, P, k, sb, pst, locals())
        phase_outproj_mlp("1", YC1_d, B_YC1, N1, WOb[1], g_mixpost[1], X2_d[128:N0 * 128, :], B_X2, X3_d, B_X3,
                          out, B_OUT, W1b[1], W2b[1], g_mlppre[1], g_mlppost[1], B_WOb[1], B_W1b[1], B_W2b[1])

        if dbg_o is not None:
            pass
        waits = []
        for p_, v_ in B_OUT.writers.items():
            P._need("sync", p_, v_, waits)
        P.cnt["sync"] += 1
        P.ops["sync"].append([waits, None, P.cnt["sync"], None])
        P.barrier(); P.flush()
    return nc


def trig_turns(k, X, n, tmpI, tmpA, tmpB, cosT, sinT, R, Bt, Bout):
    k.v("tensor_copy", tmpI, X, R=R + [Bt], W=[Bt])
    k.v("tensor_copy", tmpA, tmpI, R=[Bt], W=[Bt])
    k.v("tensor_tensor", X, X, tmpA, ALU.subtract, R=[Bt], W=[Bt])
    k.v("scalar_tensor_tensor", tmpA, X, 0.5, X, ALU.is_gt, ALU.subtract, R=[Bt], W=[Bt])
    k.v("scalar_tensor_tensor", X, tmpA, 0.5, tmpA, ALU.is_gt, ALU.subtract, R=[Bt], W=[Bt])
    k.act(sinT, X, AF.Sin, [Bt], [Bout], scale=TWO_PI)
    k.v("scalar_tensor_tensor", tmpB, X, -1.0, X, ALU.mult, ALU.max, R=[Bt], W=[Bt])
    k.act(cosT, tmpB, AF.Sin, [Bt], [Bout], scale=-TWO_PI, bias=math.pi / 2)


def trig_turns_fast(k, X, tmpI, tmpA, cosT, sinT, Bt, Bout, eng2="gpsimd"):
    k.v("tensor_copy", tmpI, X, R=[Bt], W=[Bt])
    k.v("tensor_copy", tmpA, tmpI, R=[Bt], W=[Bt], eng=eng2)
    k.v("tensor_tensor", X, X, tmpA, ALU.subtract, R=[Bt], W=[Bt], eng=eng2)
    k.act(sinT, X, AF.Sin, [Bt], [Bout], scale=TWO_PI)
    k.act(tmpA, X, AF.Abs, [Bt], [Bt])
    k.act(cosT, tmpA, AF.Sin, [Bt], [Bout], scale=-TWO_PI, bias=math.pi / 2)


class S5C:
    pass


def s5_alloc(es, sb):
    C = S5C()
    C.BsT = sb(es, "BsT", [128, 4, 2, 16, 128], BF16); C.BBsT = Buf()
    C.Klag = sb(es, "Klag", [128, 4, 16, 128], BF16); C.BKl = Buf()
    C.CrEX = [sb(es, f"CrEX{c}", [128, 16, 17, 32], BF16) for c in range(2)]; C.BCr = Buf()
    C.rho16 = sb(es, "rho16", [128, 16], F32); C.f16 = sb(es, "f16t", [128, 16], F32); C.Brho = Buf()
    C.kF = sb(es, "kF", [128, 256], F32)
    C.dcol = sb(es, "dcol", [128, 4], F32); C.Bd = Buf()
    return C


def s5_const_gen(nc, P, k, sb, pst, L, es, C):
    identf, identb, maskE, maskP, B_const = L["identf"], L["identb"], L["maskE"], L["maskP"], L["B_const"]
    lam_re, lam_im, log_dt = L["lam_re"][0], L["lam_im"][0], L["log_dt"][0]
    b_re, b_im, c_re, c_im = L["b_re"][0], L["b_im"][0], L["c_re"][0], L["c_im"][0]
    s5_d = L["s5_d"][0]
    BsT, BBsT, Klag, BKl, CrEX, BCr = C.BsT, C.BBsT, C.Klag, C.BKl, C.CrEX, C.BCr
    fl = lambda t3: t3[:].rearrange("p a b -> p (a b)")
    Bt = Buf("s5tmp")
    ops = []

    def v(*a, **kw):
        kw.setdefault("R", [Bt]); kw.setdefault("W", [Bt])
        ops.append(lambda: k.v(*a, **kw))

    def act(*a, **kw):
        ops.append(lambda: k.act(*a, **kw))

    def tr(*a):
        ops.append(lambda: k.tr(*a))

    def mm(*a, **kw):
        ops.append(lambda: k.mm(*a, **kw))

    def dma(*a, **kw):
        ops.append(lambda: k.dma(*a, **kw))

    pc = pst(es, "pc1", [128, 512]); Bpc = Buf()
    pcb = pc.bitcast(BF16); Bpcb = Bpc
    C.pcb, C.Bpcb = pcb, Bpcb
    src16 = sb(es, "src16", [16, 3, 128], F32)
    ld16 = sb(es, "ld16", [16, 2], F32)
    PL = sb(es, "PL", [128, 3, 16], F32)
    d4 = sb(es, "d4", [4, 128], F32)
    sc = sb(es, "sc", [128, 12, 16], F32)
    kI = sb(es, "kI", [128, 256], I32); kF = C.kF
    mag = sb(es, "mag", [128, 16, 17], F32); ang = sb(es, "ang", [128, 16, 17], F32)
    Pre = sb(es, "Pre", [128, 16, 17], F32); Pim = sb(es, "Pim", [128, 16, 17], F32)
    Qre = sb(es, "Qre", [128, 16, 16], F32); Qim = sb(es, "Qim", [128, 16, 16], F32); Qt = sb(es, "Qt", [128, 16, 16], F32)
    tI = sb(es, "tI", [128, 272], I32); tA = sb(es, "tA", [128, 272], F32); tB = sb(es, "tB", [128, 272], F32)
    dma("sync", src16[:, 0, :], lam_re.rearrange("(pr e) p -> pr (e p)", e=2), [], [Bt])
    dma("sync", src16[:, 1, :], lam_im.rearrange("(pr e) p -> pr (e p)", e=2), [], [Bt])
    dma("sync", ld16[:], log_dt.rearrange("(pr e) -> pr e", e=2), [], [Bt])
    dma("sync", d4[:], s5_d.rearrange("(t p) -> t p", p=128), [], [Bt])
    v("tensor_copy", src16[:, 2, :].rearrange("q (e p) -> q e p", e=2), bcast(ld16[:], 2, 64))
    tr(pc[:, 64:68], d4[:], identf[0:4, 0:4], [Bt, B_const], [Bpc])
    v("tensor_copy", C.dcol[:], pc[:, 64:68], R=[Bpc], W=[C.Bd])
    for i in range(3):
        tr(pc[:, i * 16:(i + 1) * 16], src16[:, i, :], identf[0:16, 0:16], [Bt, B_const], [Bpc])
    v("tensor_copy", PL[:], pc[:, 0:48].rearrange("p (a b) -> p a b", a=3), R=[Bpc], W=[Bt])
    lr, li = PL[:, 0, :], PL[:, 1, :]
    dt_, A_, TH_ = sc[:, 0, :], sc[:, 1, :], sc[:, 2, :]
    act(dt_, PL[:, 2, :], AF.Exp, [Bt], [Bt])
    v("tensor_tensor", A_, lr, dt_, ALU.mult)
    v("tensor_tensor", TH_, li, dt_, ALU.mult)
    v("tensor_scalar", TH_, TH_, 1.0 / TWO_PI, None, ALU.mult)
    v("iota", kI[:], [[1, 256]], base=0, channel_multiplier=0, eng="gpsimd", R=[], W=[Bt])
    v("tensor_copy", kF[:], kI[:], W=[Bt, C.Brho])
    kF17 = bcast(kF[:, 0:17], 1, 16)
    v("tensor_tensor", mag[:], bcast(A_, 2, 17), kF17, ALU.mult)
    act(mag[:], mag[:], AF.Exp, [Bt], [Bt])
    v("tensor_tensor", ang[:], bcast(TH_, 2, 17), kF17, ALU.mult)
    ops.append(lambda: trig_turns(k, fl(ang), 272, tI[:], tA[:], tB[:], fl(Pre), fl(Pim), [], Bt, Bt))
    v("tensor_tensor", Pre[:], Pre[:], mag[:], ALU.mult)
    v("tensor_tensor", Pim[:], Pim[:], mag[:], ALU.mult)
    v("tensor_copy", C.rho16[:], mag[:, :, 16], W=[C.Brho])
    f16b = sc[:, 4, :]
    v("tensor_scalar", C.f16[:], TH_, 16.0, None, ALU.mult, W=[C.Brho])
    v("tensor_copy", tI[:, 0:16], C.f16[:], R=[C.Brho])
    v("tensor_copy", f16b, tI[:, 0:16])
    v("tensor_tensor", C.f16[:], C.f16[:], f16b, ALU.subtract, R=[Bt, C.Brho], W=[C.Brho])
    nr, den, rden, qre, qim, t1, t2 = (sc[:, i, :] for i in range(5, 12))
    are, aim = Pre[:, :, 1], Pim[:, :, 1]
    v("tensor_scalar", nr, are, -1.0, None, ALU.add)
    v("tensor_tensor", den, lr, lr, ALU.mult)
    v("tensor_tensor", t1, li, li, ALU.mult)
    v("tensor_tensor", den, den, t1, ALU.add)
    v("reciprocal", rden, den)
    v("tensor_tensor", t1, nr, lr, ALU.mult)
    v("tensor_tensor", t2, aim, li, ALU.mult)
    v("tensor_tensor", t1, t1, t2, ALU.add)
    v("tensor_tensor", qre, t1, rden, ALU.mult)
    v("tensor_tensor", t1, aim, lr, ALU.mult)
    v("tensor_tensor", t2, nr, li, ALU.mult)
    v("tensor_tensor", t1, t1, t2, ALU.subtract)
    v("tensor_tensor", qim, t1, rden, ALU.mult)
    P16r, P16i = Pre[:, :, 0:16], Pim[:, :, 0:16]
    v("tensor_tensor", Qre[:], bcast(qre, 2, 16), P16r, ALU.mult)
    v("tensor_tensor", Qt[:], bcast(qim, 2, 16), P16i, ALU.mult)
    v("tensor_tensor", Qre[:], Qre[:], Qt[:], ALU.subtract)
    v("tensor_tensor", Qim[:], bcast(qre, 2, 16), P16i, ALU.mult)
    v("tensor_tensor", Qt[:], bcast(qim, 2, 16), P16r, ALU.mult)
    v("tensor_tensor", Qim[:], Qim[:], Qt[:], ALU.add)
    n_eager = len(ops)
    Bre = sb(es, "Bre", [128, 16, 16], F32); Bim = sb(es, "Bim", [128, 16, 16], F32)
    dma("sync", Bre[:], b_re.rearrange("(pr e) p h -> (e p) pr h", e=2), [], [Bt])
    dma("sync", Bim[:], b_im.rearrange("(pr e) p h -> (e p) pr h", e=2), [], [Bt])
    BsF = sb(es, "BsF", [128, 4, 16, 16], F32); BsG = sb(es, "BsG", [128, 4, 16, 16], F32)
    EX1 = [sb(es, f"EX1_{i}", [128, 16, 4, 2, 16], BF16) for i in range(2)]
    EXP = [sb(es, f"EXP{c}", [128, 16, 4, 32], BF16) for c in range(2)]
    Bbs = Buf(); Bex = [Buf(), Buf()]; Bexp = Buf()
    for c in range(2):
        v("memset", EXP[c][:], 0.0, R=[], W=[Bexp], eng="gpsimd")
    blocks = []
    for gt in range(4):
        g4 = slice(gt * 4, (gt + 1) * 4)
        Qr4, Qi4 = bcast(Qre[:, g4, :], 3, 16), bcast(Qim[:, g4, :], 3, 16)
        Br4, Bi4 = bcast(Bre[:, g4, :], 2, 16), bcast(Bim[:, g4, :], 2, 16)
        for c in range(2):
            bi = len(blocks)
            E1 = EX1[bi % 2]; BE = Bex[bi % 2]
            n0 = len(ops)
            if c == 0:
                v("tensor_tensor", BsF[:], Qr4, Br4, ALU.mult, R=[Bt], W=[Bbs])
                v("tensor_tensor", BsG[:], Qi4, Bi4, ALU.mult, R=[Bt], W=[Bbs], eng="gpsimd")
                v("tensor_tensor", BsF[:], BsF[:], BsG[:], ALU.subtract, R=[Bbs], W=[Bbs])
            else:
                v("tensor_tensor", BsF[:], Qr4, Bi4, ALU.mult, R=[Bt], W=[Bbs])
                v("tensor_tensor", BsG[:], Qi4, Br4, ALU.mult, R=[Bt], W=[Bbs], eng="gpsimd")
                v("tensor_tensor", BsF[:], BsF[:], BsG[:], ALU.add, R=[Bbs], W=[Bbs])
            for e_ in range(2):
                v("tensor_scalar", E1[:, :, :, e_, :], BsF[:].rearrange("p a b h -> p b a h"),
                  maskE[:, e_:e_ + 1], None, ALU.mult, R=[Bbs, B_const], W=[BE])
            for a_ in range(4):
                v("tensor_copy", EXP[c][:, gt * 4 + a_, a_, :], E1[:, 0, a_, :, :].rearrange("p e h -> p (e h)"),
                  R=[BE], W=[Bexp])
            n1 = len(ops)
            for k0 in range(0, 16, 8):
                for kk in range(8):
                    tr(pcb[:, kk * 128:(kk + 1) * 128], E1[:, k0 + kk, :, :, :].rearrange("p a e h -> p (a e h)"),
                       identb[:], [BE, B_const], [Bpcb])
                v("tensor_copy", BsT[:, gt, c, k0:k0 + 8, :], pcb[:].rearrange("p (a b) -> p a b", a=8),
                  R=[Bpcb], W=[BBsT])
            n2 = len(ops)
            blocks.append((ops[n0:n1], ops[n1:n2]))
            del ops[n0:]
    ops.extend(blocks[0][0])
    for bi in range(len(blocks)):
        if bi + 1 < len(blocks):
            ops.extend(blocks[bi + 1][0])
        ops.extend(blocks[bi][1])
    SRa = sb(es, "SRa", [128, 4, 128], F32); SRb = sb(es, "SRb", [128, 4, 128], F32)
    cr4 = c_re.rearrange("(gt g8) h p -> (g8 h) gt p", gt=4)
    ci4 = c_im.rearrange("(gt g8) h p -> (g8 h) gt p", gt=4)
    dma("sync", SRa[:, :, 0:64], cr4, [], [Bt]); dma("sync", SRa[:, :, 64:128], ci4, [], [Bt])
    dma("sync", SRb[:, :, 0:64], ci4, [], [Bt]); dma("sync", SRb[:, :, 64:128], cr4, [], [Bt])
    Ta = sb(es, "Ta", [128, 4, 128], F32); Tb = sb(es, "Tb", [128, 4, 128], F32)
    for gt in range(4):
        tr(pc[:, gt * 128:(gt + 1) * 128], SRa[:, gt, :], identf[:], [Bt, B_const], [Bpc])
    v("tensor_copy", Ta[:], pc[:].rearrange("p (a b) -> p a b", a=4), R=[Bpc])
    for gt in range(4):
        tr(pc[:, gt * 128:(gt + 1) * 128], SRb[:, gt, :], identf[:], [Bt, B_const], [Bpc])
    v("tensor_copy", Tb[:], pc[:].rearrange("p (a b) -> p a b", a=4), R=[Bpc])
    CTre = sb(es, "CTre", [128, 16, 16], F32); CTim = sb(es, "CTim", [128, 16, 16], F32)
    va = Ta[:].rearrange("p g (a e h) -> p (g a) e h", a=4, e=2)
    vb = Tb[:].rearrange("p g (a e h) -> p (g a) e h", a=4, e=2)
    m0, m1 = maskE[:, 0:1], maskE[:, 1:2]
    v("tensor_scalar", CTre[:], va[:, :, 0, :], m0, None, ALU.mult, R=[Bt, B_const])
    v("scalar_tensor_tensor", CTre[:], vb[:, :, 1, :], m1, CTre[:], ALU.mult, ALU.add, R=[Bt, B_const])
    v("tensor_scalar", CTim[:], vb[:, :, 0, :], m0, None, ALU.mult, R=[Bt, B_const])
    v("scalar_tensor_tensor", CTim[:], va[:, :, 1, :], m1, CTim[:], ALU.mult, ALU.add, R=[Bt, B_const])
    CrF = sb(es, "CrF", [128, 4, 17, 16], F32); CrG = sb(es, "CrG", [128, 4, 17, 16], F32)
    Bcf = Buf(); Bcg = [Buf() for _ in range(4)]
    cblocks = []
    for gt in range(4):
        g4 = slice(gt * 4, (gt + 1) * 4)
        Pr4, Pi4 = bcast(Pre[:, g4, :], 3, 16), bcast(Pim[:, g4, :], 3, 16)
        Cr4, Ci4 = bcast(CTre[:, g4, :], 2, 17), bcast(CTim[:, g4, :], 2, 17)
        n0 = len(ops)
        for c in range(2):
            if c == 0:
                v("tensor_tensor", CrF[:], Cr4, Pr4, ALU.mult, R=[Bt], W=[Bcf])
                v("tensor_tensor", CrG[:], Ci4, Pi4, ALU.mult, R=[Bt], W=[Bcf], eng="gpsimd")
                v("tensor_tensor", CrF[:], CrF[:], CrG[:], ALU.subtract, R=[Bcf], W=[Bcf])
            else:
                v("tensor_tensor", CrF[:], Cr4, Pi4, ALU.mult, R=[Bt], W=[Bcf])
                v("tensor_tensor", CrG[:], Ci4, Pr4, ALU.mult, R=[Bt], W=[Bcf], eng="gpsimd")
                v("scalar_tensor_tensor", CrF[:], CrF[:], -1.0, CrG[:], ALU.mult, ALU.subtract, R=[Bcf], W=[Bcf])
            src = CrF[:].rearrange("p a b h -> p (a b) h")
            v("tensor_tensor", CrEX[c][:, g4, :, :].rearrange("p a b (e h) -> p (a b) e h", e=2), bcast(src, 2, 2),
              bcast(bcast(maskE[:], 1, 68), 3, 16), ALU.mult, R=[Bcf, B_const], W=[Bcg[gt]])
        n1 = len(ops)
        for a_ in range(4):
            pr = gt * 4 + a_
            for c in range(2):
                mm(pc[:, :], EXP[c][:, pr, :, :].rearrange("p a h -> p (a h)"),
                   CrEX[c][:, pr, 0:16, :].rearrange("p a h -> p (a h)"),
                   (a_ == 0 and c == 0), (a_ == 3 and c == 1), [Bexp, Bcg[gt]], [Bpc])
        srcp = pc[:].rearrange("p (j h) -> p j h", j=16)
        v("tensor_tensor", Klag[:, gt, :, :].rearrange("p j (a h) -> p j a h", a=4), bcast(srcp, 2, 4),
          bcast(bcast(maskP[:], 1, 16), 3, 32), ALU.mult, R=[Bpc, B_const], W=[BKl])
        n2 = len(ops)
        cblocks.append((ops[n0:n1], ops[n1:n2]))
        del ops[n0:]
    ops.extend(cblocks[0][0])
    for bi in range(len(cblocks)):
        if bi + 1 < len(cblocks):
            ops.extend(cblocks[bi + 1][0])
        ops.extend(cblocks[bi][1])
    def run():
        for o in ops:
            o()
            yield
    g = run()
    for _ in range(n_eager):
        next(g)
    return g


def s5_runtime(nc, P, k, sb, pst, L, C):
    UT_d, B_UT, YC_d, B_YC = L["UT_d"], L["B_UT"], L["YC_d"], L["B_YC"]
    w_glu = L["w_glu"][0]
    BsT, BBsT, Klag, BKl, CrEX, BCr, rho16, Brho = C.BsT, C.BBsT, C.Klag, C.BKl, C.CrEX, C.BCr, C.rho16, C.Brho
    dcol, Bd = C.dcol, C.Bd
    with contextlib.ExitStack() as e1:
        Xs = sb(e1, "Xs", [128, 16, 2, 260], BF16); BXs = Buf()
        wg = sb(e1, "wg", [128, 4, 512], BF16); Bwg = Buf()
        for ft in range(4):
            k.dma("gpsimd", wg[:, ft, :], w_glu[ft * 128:(ft + 1) * 128, :], [], [Bwg])
        with contextlib.ExitStack() as e2:
            UTs = sb(e2, "UTs", [128, 4, 16, 256], BF16); BUT = Buf()
            tmpU = [sb(e2, f"tmpU{i}", [128, NT * 128], BF16) for i in range(2)]; BtU = [Buf(), Buf()]
            for ft in range(4):
                k.dma("sync", tmpU[ft % 2][:], UT_d[ft], [B_UT], [BtU[ft % 2]])
                srcv = tmpU[ft % 2][:].rearrange("p (m s) -> p s m", s=16)
                if ft % 2 == 0:
                    k.v("tensor_copy", UTs[:, ft, :, :], srcv, R=[BtU[ft % 2]], W=[BUT])
                else:
                    k.act(UTs[:, ft, :, :], srcv, AF.Copy, [BtU[ft % 2]], [BUT])
            NI = 3
            s1 = [pst(e2, f"s1p{i}", [128, 2, 256]) for i in range(NI)]; Bs1 = [Buf() for _ in range(NI)]
            tw = [sb(e2, f"tw{i}", [128, 6, 256], F32) for i in range(NI)]
            Btw = [Buf() for _ in range(NI)]; Btw2 = [Buf() for _ in range(NI)]; Btwz = [Buf() for _ in range(NI)]
            sev = [sb(e2, f"sev{i}", [128, 2, 256], F32) for i in range(NI)]; Bsev = [Buf() for _ in range(NI)]
            tg = [sb(e2, f"tg{i}", [128, 5, 256], F32) for i in range(NI)]; Btg = [Buf() for _ in range(NI)]
            tgI = [sb(e2, f"tgI{i}", [128, 256], I32) for i in range(NI)]
            k.v("memset", Xs[:, :, :, 0:1], 0.0, W=[BXs])

            def pair_ops(pr, i):
                gt, r0 = pr // 4, (pr % 4) * 32
                G_ = tg[i]; BG = Btg[i]
                BXa, BXb = Buf(), Buf()
                k.v("tensor_scalar", G_[:, 0, :], C.kF[:, 0:256], C.f16[:, pr:pr + 1], None, ALU.mult, R=[Brho], W=[BG]); yield
                k.v("tensor_copy", tgI[i][:], G_[:, 0, :], R=[BG], W=[BG]); yield
                k.v("tensor_copy", G_[:, 1, :], tgI[i][:], R=[BG], W=[BG], eng="gpsimd"); yield
                k.v("tensor_tensor", G_[:, 0, :], G_[:, 0, :], G_[:, 1, :], ALU.subtract, R=[BG], W=[BG], eng="gpsimd"); yield
                k.act(G_[:, 4, :], G_[:, 0, :], AF.Sin, [BG], [BG], scale=TWO_PI); yield
                k.act(G_[:, 1, :], G_[:, 0, :], AF.Abs, [BG], [BG]); yield
                k.act(G_[:, 3, :], G_[:, 1, :], AF.Sin, [BG], [BG], scale=-TWO_PI, bias=math.pi / 2); yield
                cs, sn = G_[:, 3, :], G_[:, 4, :]
                for c in range(2):
                    for s in range(16):
                        k.mm(s1[i][:, c, :], BsT[r0:r0 + 32, gt, c, 15 - s, :], UTs[r0:r0 + 32, gt, s, :],
                             s == 0, s == 15, [BBsT, BUT], [Bs1[i]], tile_position=(r0, 0))
                    yield
                T = tw[i]
                BTa, BTb, BTz = Btw[i], Btw2[i], Btwz[i]
                se = sev[i]; Bse = Bsev[i]
                k.act(se[:], s1[i][:], AF.Copy, [Bs1[i]], [Bse]); yield
                R0 = [Bse, BG]
                k.v("tensor_tensor", T[:, 0, :], se[:, 0, :], cs, ALU.mult, R=R0, W=[BTa])
                k.v("tensor_tensor", T[:, 2, :], se[:, 1, :], cs, ALU.mult, R=R0, W=[BTb], eng="gpsimd"); yield
                k.v("tensor_tensor", T[:, 1, :], se[:, 1, :], sn, ALU.mult, R=R0, W=[BTa])
                k.v("tensor_tensor", T[:, 3, :], se[:, 0, :], sn, ALU.mult, R=R0, W=[BTb], eng="gpsimd"); yield
                k.v("tensor_tensor", T[:, 0, :], T[:, 0, :], T[:, 1, :], ALU.add, R=[BTa], W=[BTa])
                k.v("tensor_tensor", T[:, 2, :], T[:, 2, :], T[:, 3, :], ALU.subtract, R=[BTb], W=[BTb], eng="gpsimd"); yield
                rb = rho16[:, pr:pr + 1].to_broadcast([128, 256])
                k.v("tensor_tensor_scan", T[:, 4, :], rb, T[:, 0, :], 0.0, ALU.mult, ALU.add, R=[BTa, Brho], W=[BTz]); yield
                k.v("tensor_tensor_scan", T[:, 5, :], rb, T[:, 2, :], 0.0, ALU.mult, ALU.add, R=[BTb, Brho], W=[BTz]); yield
                k.v("tensor_tensor", T[:, 0, :], T[:, 4, :], cs, ALU.mult, R=[BTz, BG], W=[BTa], eng="gpsimd")
                k.v("tensor_tensor", T[:, 2, :], T[:, 4, :], sn, ALU.mult, R=[BTz, BG], W=[BTb]); yield
                k.v("tensor_tensor", T[:, 1, :], T[:, 5, :], sn, ALU.mult, R=[BTz, BG], W=[BTa], eng="gpsimd")
                k.v("tensor_tensor", T[:, 3, :], T[:, 5, :], cs, ALU.mult, R=[BTz, BG], W=[BTb]); yield
                k.v("tensor_tensor", Xs[:, pr, 0, 1:257], T[:, 0, :], T[:, 1, :], ALU.subtract, R=[BTa], W=[BXa], eng="gpsimd")
                k.v("tensor_tensor", Xs[:, pr, 1, 1:257], T[:, 2, :], T[:, 3, :], ALU.add, R=[BTb], W=[BXb]); yield

            for q in range((16 + NI - 1) // NI):
                gens = [pair_ops(q * NI + j, j) for j in range(NI) if q * NI + j < 16]
                live = list(gens)
                while live:
                    for g_ in list(live):
                        try:
                            next(g_)
                        except StopIteration:
                            live.remove(g_)
            P.barrier(); P.flush()

        with contextlib.ExitStack() as e2:
            UT = sb(e2, "UT", [128, 4, NT * 128], BF16); BUT = Buf()
            for ft in range(4):
                k.dma("sync", UT[:, ft, :], UT_d[ft], [B_UT], [BUT])
            yp = [pst(e2, f"yp{i}", [128, 512]) for i in range(2)]; Byp = [Buf(), Buf()]
            gp_ = [pst(e2, f"gp{i}", [128, 512]) for i in range(2)]; Bgp = [Buf(), Buf()]
            CB = pst(e2, "CBp", [128, 2048]); BCBq = [Buf() for _ in range(4)]
            CBs = [sb(e2, f"CBs{i}", [32, 2048], BF16) for i in range(2)]; BCBs = [Buf(), Buf()]
            identb, B_const = L["identb"], L["B_const"]
            yf = [sb(e2, f"yf{i}", [128, 512], F32) for i in range(2)]; Byf = [Buf(), Buf()]
            YG = sb(e2, "YG", [128, 4, 512], BF16); BYG = [Buf() for _ in range(4)]
            SG = [sb(e2, f"SG{i}", [128, 512], BF16) for i in range(2)]; BSG = [Buf(), Buf()]
            YA = [sb(e2, f"YA{i}", [128, 4, 512], BF16) for i in range(2)]; BYA = [Buf(), Buf()]
            groups = [(T0, 1)] + [(t, 4) for t in range(T1, NT, 4)]
            n = 0
            for gi, (t0, nt) in enumerate(groups):
                N = nt * 128
                nb = N // 16
                tok0 = t0 * 128
                b0 = tok0 // 16
                for gt in range(4):
                    i = n % 2; n += 1
                    Y = yp[i]; BY = Byp[i]
                    Y3 = Y[:, 0:N].rearrange("p (m r) -> p m r", r=16)
                    U3 = UT[:, gt, tok0:tok0 + N].rearrange("p (m r) -> p m r", r=16)
                    ci = (n // 1) % 2
                    for q_ in range(4):
                        for a_ in range(4):
                            pr = gt * 4 + a_
                            for c in range(2):
                                k.mm(CB[0:nb, q_ * 512:(q_ + 1) * 512].rearrange("p (r x) -> p r x", r=4)[:, :, a_ * 32:(a_ + 1) * 32],
                                     Xs[:, pr, c, b0:b0 + nb],
                                     CrEX[c][:, pr, 4 * q_ + 1:4 * q_ + 5, :], c == 0, c == 1, [BXs, BCr], [BCBq[q_]])
                        if q_ % 2 == 0:
                            k.v("tensor_copy", CBs[ci][0:nb, q_ * 512:(q_ + 1) * 512], CB[0:nb, q_ * 512:(q_ + 1) * 512],
                                R=[BCBq[q_]], W=[BCBs[ci]])
                        else:
                            k.act(CBs[ci][0:nb, q_ * 512:(q_ + 1) * 512], CB[0:nb, q_ * 512:(q_ + 1) * 512], AF.Copy,
                                  [BCBq[q_]], [BCBs[ci]])
                    k.mm(Y[:, 0:N], Klag[:, gt, 0, :], UT[:, gt, tok0:tok0 + N], True, False, [BKl, BUT], [BY],
                         skip_group_check=True)
                    for j in range(1, 16):
                        k.mm(Y3[:, :, j:16], Klag[:, gt, j, :], U3[:, :, 0:16 - j], False, False, [BKl, BUT], [BY],
                             skip_group_check=True)
                    for r in range(16):
                        k.mm(Y3[:, :, r], CBs[ci][0:nb, r * 128:(r + 1) * 128], identb[0:nb, 0:nb], False, r == 15,
                             [BCBs[ci], B_const], [BY], skip_group_check=True)
                    k.v("scalar_tensor_tensor", yf[i][:, 0:N], UT[:, gt, tok0:tok0 + N], dcol[:, gt:gt + 1], Y[:, 0:N],
                        ALU.mult, ALU.add, R=[BUT, Bd, BY], W=[Byf[i]])
                    k.act(YG[:, gt, 0:N], yf[i][:, 0:N], AF.Gelu_apprx_tanh, [Byf[i]], [BYG[gt]])
                ai = gi % 2
                for fo in range(4):
                    i = n % 2; n += 1
                    for ft in range(4):
                        k.mm(gp_[i][:, 0:N], wg[:, ft, fo * 128:(fo + 1) * 128], YG[:, ft, 0:N], ft == 0, ft == 3,
                             [Bwg, BYG[ft]], [Bgp[i]])
                    k.act(SG[i][:, 0:N], gp_[i][:, 0:N], AF.Sigmoid, [Bgp[i]], [BSG[i]])
                    k.v("tensor_tensor", YA[ai][:, fo, 0:N], YG[:, fo, 0:N], SG[i][:, 0:N], ALU.mult,
                        R=[BYG[fo], BSG[i]], W=[BYA[ai]], eng="gpsimd")
                c0 = (t0 - T0) * 128
                k.dma("sync", YC_d[0:4, :, c0:c0 + N].rearrange("f p t -> p f t"), YA[ai][:, :, 0:N], [BYA[ai]], [B_YC])
            P.barrier(); P.flush()


def phase_fox(nc, P, k, sb, pst, L, bg=None):
    identb, maskb, Fkb, B_const, B_Fkb = L["identb"], L["maskb"], L["Fkb"], L["B_const"], L["B_Fkb"]
    KT_d, B_KT, QT_d, B_QT, V_d, B_V, YC_d, B_YC = (L[x] for x in ["KT_d", "B_KT", "QT_d", "B_QT", "V_d", "B_V", "YC_d", "B_YC"])
    NS = 3
    with contextlib.ExitStack() as es:
        Vh = [sb(es, f"Vh{i}", [128, NT, 68], BF16) for i in range(2)]; BVh = [Buf(), Buf()]
        KTa = [sb(es, f"KTa{i}", [67, NT * 128], BF16) for i in range(2)]; BKa = [Buf(), Buf()]
        QTa = [sb(es, f"QTa{i}", [67, N0 * 128], BF16) for i in range(2)]; BQa = [Buf(), Buf()]
        YB = sb(es, "YB", [128, N0, 512], BF16); BYB = Buf()
        Pt = [sb(es, f"Pt{i}", [128, 512], BF16) for i in range(NS)]; BPt = [Buf() for _ in range(NS)]
        dn = sb(es, "dn", [128, 8], F32); Bdn = Buf()
        S = [pst(es, f"Sps{i}", [128, 512]) for i in range(NS)]; BS = [Buf() for _ in range(NS)]
        O = [pst(es, f"Ops{i}", [128, 512]) for i in range(4)]; BO = [Buf() for _ in range(4)]
        for i in range(2):
            k.v("memset", KTa[i][64:67, :], 1.0, W=[BKa[i]])

        def load_head(h):
            hi = h % 2
            k.dma("sync", KTa[hi][0:64, :], KT_d[h], [B_KT], [BKa[hi]])
            k.dma("sync", QTa[hi][:, :], QT_d[h], [B_QT], [BQa[hi]])
            k.dma("sync", Vh[hi][:], V_d[:, :, h * 68:(h + 1) * 68].rearrange("t p c -> p t c"), [B_V], [BVh[hi]])

        load_head(0)
        L["convert_all_weights"]()
        conv_pend = L["conv_pend"]
        groups = [(T0, 1)] + [(t, 4) for t in range(T1, NT, 4)]
        its = [(h, g0, nq, kt) for h in range(8) for (g0, nq) in groups for kt in range(g0 + nq)]

        def issue_S(n):
            h, g0, nq, kt = its[n]
            hi = h % 2
            q0 = (g0 - T0) * 128
            i0 = max(0, kt - g0)
            si = n % NS
            diag = kt >= g0
            k.mm(S[si][:, i0 * 128:nq * 128], KTa[hi][:, kt * 128:(kt + 1) * 128],
                 QTa[hi][:, q0 + i0 * 128:q0 + nq * 128], True, not diag, [BKa[hi], BQa[hi]], [BS[si]])
            if diag:
                k.mm(S[si][:, i0 * 128:(i0 + 1) * 128], identb[:], maskb[:], False, True, [B_const], [BS[si]])

        LOOK = 2
        for n in range(min(LOOK, len(its))):
            issue_S(n)
        for n, (h, g0, nq, kt) in enumerate(its):
            if n + LOOK < len(its):
                h2 = its[n + LOOK][0]
                if h2 != its[n + LOOK - 1][0]:
                    pass
                issue_S(n + LOOK)
            if kt == 0 and g0 == T0 and h + 1 < 8:
                load_head(h + 1)
            if bg is not None:
                next(bg, None)
            if conv_pend and n % 40 == 12:
                conv_pend.pop(0)()
            i0 = max(0, kt - g0)
            si = n % NS
            k.act(Pt[si][:, i0 * 128:nq * 128], S[si][:, i0 * 128:nq * 128], AF.Exp, [BS[si], B_Fkb], [BPt[si]],
                  scale=0.125, bias=Fkb[:, kt, h:h + 1])
            for i in range(i0, nq):
                last = (kt == g0 + i)
                k.mm(O[i][:, 0:65], Pt[si][:, i * 128:(i + 1) * 128], Vh[h % 2][:, kt, 0:65],
                     kt == 0, last, [BPt[si], BVh[h % 2]], [BO[i]])
                if last:
                    k.v("tensor_scalar", dn[:, i:i + 1], O[i][:, 64:65], 1e-30, None, ALU.max, R=[BO[i]], W=[Bdn])
                    k.v("reciprocal", dn[:, 4 + i:5 + i], dn[:, i:i + 1], R=[Bdn], W=[Bdn])
                    k.v("tensor_scalar", YB[:, g0 - T0 + i, h * 64:(h + 1) * 64], O[i][:, 0:64],
                        dn[:, 4 + i:5 + i], None, ALU.mult, R=[BO[i], Bdn], W=[BYB])
        while conv_pend:
            conv_pend.pop(0)()
        if bg is not None:
            for _ in bg:
                pass
        tp, Btp = L["C5"].pcb, L["C5"].Bpcb
        yt = [sb(es, f"ytF{i}", [128, 4, 128], BF16) for i in range(2)]; Byt = [Buf(), Buf()]
        for t in range(N0):
            i = t % 2
            for ft in range(4):
                k.tr(tp[:, ft * 128:(ft + 1) * 128], YB[:, t, ft * 128:(ft + 1) * 128], identb[:], [BYB, B_const], [Btp])
            k.v("tensor_copy", yt[i][:], tp[:, 0:512].rearrange("p (f t) -> p f t", f=4), R=[Btp], W=[Byt[i]])
            k.dma("gpsimd", YC_d[4:8, :, t * 128:(t + 1) * 128].rearrange("f p t -> p f t"), yt[i][:], [Byt[i]], [B_YC])
        P.barrier(); P.flush()


def phase_odd(nc, P, k, sb, pst, L):
    identb, tril, B_const = L["identb"], L["tril"], L["B_const"]
    X2_d, B_X2, YC1_d, B_YC1 = L["X2_d"], L["B_X2"], L["YC1_d"], L["B_YC1"]
    norm_in, to_fm, load_w_bf16, load_bcast_row, rms_rstd = (L[x] for x in
        ["norm_in", "to_fm", "load_w_bf16", "load_bcast_row", "rms_rstd"])
    WO = L["w_in_o"][0]
    pool_w, pool_scale, ln_g, ln_b, w_s, b_s = (L[x][0] for x in ["pool_w", "pool_scale", "ln_g", "ln_b", "w_s", "b_s"])
    g_row = L["g_mixpre"][1]
    icnt = L["icnt"]
    NTOK = N0 * 128
    with contextlib.ExitStack() as es:
        wi = sb(es, "wiO", [128, 8, 1536], BF16); Bwi = Buf()
        gp = sb(es, "gO", [128, D], F32); Bg = Buf()
        lg = sb(es, "lgO", [128, 512], F32); lb = sb(es, "lbO", [128, 512], F32)
        pw = sb(es, "pwO", [128, 4, 128], BF16); psc = sb(es, "pscO", [128, 4], F32)
        bs_ = sb(es, "bsO", [128, 4], F32); ic = sb(es, "icO", [128, 4, 16], F32)
        WsT = sb(es, "WsT", [128, 4, 128], BF16); BWs = Buf()
        load_bcast_row(gp, Bg, g_row, D)
        load_bcast_row(lg, Bg, ln_g, 512)
        load_bcast_row(lb, Bg, ln_b, 512)
        load_w_bf16(wi, Bwi, L["WIOb"], 8, 1536, srcbuf=L["B_WIOb"], queue="sync")
        k.dma("gpsimd", pw[:], pool_w.rearrange("g c d -> c g d"), [], [Bg])
        k.dma("sync", ic[:], icnt, [], [Bg])
        XC = sb(es, "XC", [128, 4, NTOK], F32); BXC = Buf()
        tp = pst(es, "tpO", [128, 1024], BF16); Btp = Buf()
        pp = [pst(es, f"ppO{i}", [128, 512]) for i in range(4)]; Bpp = [Buf() for _ in range(4)]
        with contextlib.ExitStack() as e2:
            r4 = sb(e2, "r4", [4, 2, 128], F32); Br4 = Buf()
            k.dma("sync", r4[:, 0, :], pool_scale.rearrange("(g p) -> g p", p=128), [], [Br4])
            k.dma("sync", r4[:, 1, :], b_s, [], [Br4])
            k.tr(pp[0][:, 0:4], r4[:, 0, :], L["identf"][0:4, 0:4], [Br4, B_const], [Bpp[0]])
            k.tr(pp[0][:, 4:8], r4[:, 1, :], L["identf"][0:4, 0:4], [Br4, B_const], [Bpp[0]])
            k.v("tensor_copy", psc[:], pp[0][:, 0:4], R=[Bpp[0]], W=[Bg])
            k.v("tensor_copy", bs_[:], pp[0][:, 4:8], R=[Bpp[0]], W=[Bg])
            wsf = sb(e2, "wsf", [128, 4, 128], F32); wsb = sb(e2, "wsb", [128, 4, 128], BF16); Bt = Buf()
            k.dma("sync", wsf[:], w_s.rearrange("g t s -> t g s"), [], [Bt])
            k.v("tensor_tensor", wsb[:], wsf[:], bcast(tril[:], 1, 4), ALU.mult, R=[Bt, B_const], W=[Bt])
            for g in range(4):
                k.tr(tp[:, g * 128:(g + 1) * 128], wsb[:, g, :], identb[:], [Bt, B_const], [Btp])
            k.v("tensor_copy", WsT[:], tp[:, 0:512].rearrange("p (g t) -> p g t", g=4), R=[Btp], W=[BWs])
            P.barrier(); P.flush()
        GT = 4
        xt = [sb(es, f"xO{i}", [128, D], F32) for i in range(4)]; Bxt = [Buf() for _ in range(4)]
        hb = [sb(es, f"hbO{i}", [128, D], BF16) for i in range(2)]; Bhb = [Buf(), Buf()]
        jk = sb(es, "jO", [128, D], BF16); Bj = Buf()
        st = [sb(es, f"sO{i}", [128, 4], F32) for i in range(2)]; Bst = [Buf(), Buf()]
        hT = [sb(es, f"hTO{i}", [128, 8, GT * 128], BF16) for i in range(2)]; BhT = [Buf(), Buf()]
        NU = 3
        ug = [sb(es, f"ugO{i}", [128, 512], BF16) for i in range(NU)]; Bug = [Buf() for _ in range(NU)]
        vg = [sb(es, f"vgO{i}", [128, 512], F32) for i in range(2)]; Bvg = [Buf(), Buf()]
        vn = [sb(es, f"vnO{i}", [128, 512], BF16) for i in range(2)]; Bvn = [Buf(), Buf()]
        bst = [sb(es, f"bstO{i}", [128, 8], F32) for i in range(2)]; Bbst = [Buf(), Buf()]
        yd = [sb(es, f"ydO{i}", [128, 512], BF16) for i in range(2)]; Byd = [Buf(), Buf()]
        yt = [sb(es, f"ytO{i}", [128, 4, 128], BF16) for i in range(2)]; Byt = [Buf(), Buf()]
        tp2, Btp2 = pp[1].bitcast(BF16), Bpp[1]
        pm1 = pst(es, "pmO0", [128, 512]); Bpm1 = Buf()
        pm_ = [pm1, pm1]; Bpm = [Bpm1, Bpm1]
        puvb = [pst(es, f"puvO{i}", [128, 512]) for i in range(2)]; Bpuvb = [Buf(), Buf()]
        ctr = {"pp": 0, "x": 0}

        def nextpp():
            return pp[0], Bpp[0]

        tile_groups = [(0, 1)] + [(1 + 4 * i, 4) for i in range(4)]

        def g0(G):
            l0, nt = tile_groups[G]
            N = nt * 128
            H, BH = hT[G % 2], BhT[G % 2]
            for j in range(nt):
                lt = l0 + j
                xi = ctr["x"] % 4; ctr["x"] += 1
                k.dma("sync", xt[xi][:], X2_d[lt * 128:(lt + 1) * 128, :], [B_X2], [Bxt[xi]])
                norm_in(xt[xi], Bxt[xi], gp, Bg, hb[j % 2], Bhb[j % 2], jk, Bj, st[j % 2], Bst[j % 2])
                to_fm(hb[j % 2], Bhb[j % 2], tp, Btp, H, BH, j * 128, j % 2 == 1)
            for ft in range(4):
                p_, Bp_ = nextpp()
                for kt in range(8):
                    k.mm(p_[:, 0:N], wi[:, kt, ft * 128:(ft + 1) * 128], H[:, kt, 0:N], kt == 0, kt == 7, [Bwi, BH], [Bp_])
                if ft % 2 == 0:
                    k.v("tensor_copy", XC[:, ft, l0 * 128:l0 * 128 + N], p_[:, 0:N], R=[Bp_], W=[BXC])
                else:
                    k.act(XC[:, ft, l0 * 128:l0 * 128 + N], p_[:, 0:N], AF.Copy, [Bp_], [BXC])

        def tinfo(o):
            return 1 + o // 4, o % 4

        NV = 5
        vgq = [sb(es, f"vgq{i}", [128, 512], F32) for i in range(NV)]; Bvgq = [Buf() for _ in range(NV)]
        vnn = [sb(es, f"vnn{i}", [128, 512], F32) for i in range(2)]; Bvnn = [Buf(), Buf()]
        bsq = [sb(es, f"bsq{i}", [128, 8], F32) for i in range(4)]; Bbsq = [Buf() for _ in range(4)]
        NUU = 8
        ugq = [sb(es, f"ugq{i}", [128, 512], BF16) for i in range(NUU)]; Bugq = [Buf() for _ in range(NUU)]
        puv = {}

        def stA(o):
            G, j = tinfo(o)
            H, BH = hT[G % 2], BhT[G % 2]
            if o % 2 == 0:
                pu, Bpu, pv, Bpv = pp[2], Bpp[2], pp[3], Bpp[3]
            else:
                pu, Bpu, pv, Bpv = puvb[0], Bpuvb[0], puvb[1], Bpuvb[1]
            for kt in range(8):
                k.mm(pu[:, :], H[:, kt, j * 128:(j + 1) * 128], wi[:, kt, 512:1024], kt == 0, kt == 7, [Bwi, BH], [Bpu])
            for kt in range(8):
                k.mm(pv[:, :], H[:, kt, j * 128:(j + 1) * 128], wi[:, kt, 1024:1536], kt == 0, kt == 7, [Bwi, BH], [Bpv])
            puv[o] = (pu, Bpu, pv, Bpv)

        def stB(o):
            pu, Bpu, pv, Bpv = puv.pop(o)
            k.act(ugq[o % NUU][:], pu[:, :], AF.Gelu_apprx_tanh, [Bpu], [Bugq[o % NUU]])
            k.act(vgq[o % NV][:], pv[:, :], AF.Gelu_apprx_tanh, [Bpv], [Bvgq[o % NV]])

        def stC(o):
            S_, BS_ = bsq[o % 4], Bbsq[o % 4]
            k.v("bn_stats", S_[:, 0:6], vgq[o % NV][:], R=[Bvgq[o % NV]], W=[BS_])
            k.v("bn_aggr", S_[:, 6:8], S_[:, 0:6], R=[BS_], W=[BS_])
            k.v("tensor_scalar", S_[:, 0:1], S_[:, 7:8], EPS, None, ALU.add, R=[BS_], W=[BS_])

        def stD(o):
            S_, BS_ = bsq[o % 4], Bbsq[o % 4]
            k.act(S_[:, 0:1], S_[:, 0:1], AF.Sqrt, [BS_], [BS_])

        def stE(o):
            S_, BS_ = bsq[o % 4], Bbsq[o % 4]
            k.v("reciprocal", S_[:, 1:2], S_[:, 0:1], R=[BS_], W=[BS_])
            k.v("tensor_scalar", vnn[o % 2][:], vgq[o % NV][:], S_[:, 6:7], S_[:, 1:2], ALU.subtract, ALU.mult,
                R=[Bvgq[o % NV], BS_], W=[Bvnn[o % 2]])

        def stF(o):
            k.v("tensor_tensor", vnn[o % 2][:], vnn[o % 2][:], lg[:], ALU.mult, R=[Bvnn[o % 2], Bg], W=[Bvnn[o % 2]], eng="gpsimd")
            k.v("tensor_tensor", vn[o % 2][:], vnn[o % 2][:], lb[:], ALU.add, R=[Bvnn[o % 2], Bg], W=[Bvn[o % 2]], eng="gpsimd")

        def stG(o):
            pm, Bp = pm_[o % 2], Bpm[o % 2]
            for g in range(4):
                k.mm(pm[:, g * 128:(g + 1) * 128], WsT[:, g, :], vn[o % 2][:, g * 128:(g + 1) * 128], True, True,
                     [BWs, Bvn[o % 2]], [Bp])

        def stH(o):
            pm, Bp = pm_[o % 2], Bpm[o % 2]
            for g in range(4):
                k.v("scalar_tensor_tensor", yd[o % 2][:, g * 128:(g + 1) * 128], pm[:, g * 128:(g + 1) * 128], bs_[:, g:g + 1],
                    ugq[o % NUU][:, g * 128:(g + 1) * 128], ALU.add, ALU.mult, R=[Bp, Bg, Bugq[o % NUU]], W=[Byd[o % 2]])

        def stI(o):
            for ft in range(4):
                k.tr(tp2[:, ft * 128:(ft + 1) * 128], yd[o % 2][:, ft * 128:(ft + 1) * 128], identb[:], [Byd[o % 2], B_const], [Btp2])

        def stJ(o):
            k.act(yt[o % 2][:], tp2[:, 0:512].rearrange("p (f t) -> p f t", f=4), AF.Copy, [Btp2], [Byt[o % 2]])
            k.dma("sync", YC1_d[4:8, :, o * 128:(o + 1) * 128].rearrange("f p t -> p f t"), yt[o % 2][:], [Byt[o % 2]], [B_YC1])

        stages = [stA, stB, stC, stD, stE, stF, stG, stH, stI, stJ]

        SWW = 512 + 48
        SW = [sb(es, f"SW{i}", [128, 2, SWW], F32) for i in range(2)]; BSW = [Buf(), Buf()]
        for i in range(2):
            k.v("memset", SW[i][:], 0.0, W=[BSW[i]], eng="gpsimd")
        PLd = [sb(es, f"PLd{i}", [128, 4, 512], BF16) for i in range(2)]; BPL = [Buf(), Buf()]
        fix = sb(es, "fix", [128, 16], F32); Bfix = Buf()
        yc = [sb(es, f"ycO{i}", [128, 512], BF16) for i in range(2)]; Byc = [Buf(), Buf()]
        pctr = [0]

        def pool_group(G):
            a_ = tile_groups[G][0] * 128
            b_ = a_ + 512
            o0 = a_ - 128
            lo = a_ - 15
            pb = G % 2
            for g in range(4):
                w = 2 << g
                x = XC[:, g, :]
                eng = "vector" if (g + G) % 2 == 0 else "gpsimd"
                cur, coff = x, 0
                for s_i in range(g + 1):
                    d = 1 << s_i
                    dst = SW[g % 2][:, s_i % 2, :]
                    k.v("tensor_tensor", dst[:, 16:16 + b_ - lo], cur[:, lo - coff:b_ - coff], cur[:, lo - d - coff:b_ - d - coff],
                        ALU.add, R=[BXC, BSW[g % 2]], W=[BSW[g % 2]], eng=eng)
                    cur, coff = dst, lo - 16
                k.v("scalar_tensor_tensor", PLd[pb][:, g, :], cur[:, a_ - coff:b_ - coff], 1.0 / w, x[:, a_:b_], ALU.mult,
                    ALU.subtract, R=[BSW[g % 2], BXC], W=[BPL[pb]])
                if G == 1:
                    k.v("tensor_tensor", fix[:], cur[:, a_ - coff:a_ - coff + 16], ic[:, g, :], ALU.mult,
                        R=[BSW[g % 2], Bg], W=[Bfix])
                    k.v("tensor_tensor", PLd[pb][:, g, 0:16], fix[:], x[:, a_:a_ + 16], ALU.subtract, R=[Bfix, BXC], W=[BPL[pb]])
            for g in range(4):
                p_, Bp_ = nextpp()
                k.mm(p_[:, :], pw[:, g, :], PLd[pb][:, g, :], True, True, [Bg, BPL[pb]], [Bp_])
                i = pctr[0] % 2; pctr[0] += 1
                k.v("tensor_scalar", yc[i][:], p_[:, :], psc[:, g:g + 1], None, ALU.mult, R=[Bp_, Bg], W=[Byc[i]])
                k.dma("gpsimd", YC1_d[g, :, o0:o0 + 512], yc[i][:], [Byc[i]], [B_YC1])

        g0(0); g0(1)
        NS_ = len(stages)
        bgq = []
        for i in range(N1 + NS_ - 1):
            if i % 4 == 1 and (i // 4 + 2) < len(tile_groups):
                G2 = i // 4 + 2
                bgq.extend(k.record(lambda: g0(G2)))
                deadline = i + 2
            per = max(4, (len(bgq) + 19) // 20)
            for s_ in range(NS_ - 1, -1, -1):
                o = i - s_
                if 0 <= o < N1:
                    stages[s_](o)
                for _ in range(per):
                    if bgq:
                        bgq.pop(0)()
            if i % 4 == 3:
                while bgq:
                    bgq.pop(0)()
            if i % 4 == 3 and 1 + i // 4 < len(tile_groups):
                G1 = 1 + i // 4
                bgq.extend(k.record(lambda: pool_group(G1)))
        while bgq:
            bgq.pop(0)()
        P.barrier(); P.flush()


_W_NAMES = ["mix_pre_g", "mix_post_g", "mlp_pre_g", "mlp_post_g", "w_in_even", "s5_lam_re", "s5_lam_im", "s5_log_dt",
            "s5_b_re", "s5_b_im", "s5_c_re", "s5_c_im", "s5_d", "s5_w_glu", "fox_b_f", "w_out_even", "w_in_odd",
            "pool_w", "pool_scale", "sgu_ln_g", "sgu_ln_b", "sgu_w_s", "sgu_b_s", "w_out_odd", "mlp_w1", "mlp_w2"]


def make_in_maps(inputs):
    x = np.ascontiguousarray(np.asarray(inputs["x"], dtype=np.float32))
    shared = {n: np.ascontiguousarray(np.asarray(inputs[n], dtype=np.float32)) for n in _W_NAMES}
    maps = []
    for c in range(8):
        b, r = c // 2, c % 2
        xin = np.zeros((NT * 128, D), np.float32)
        kb = np.zeros((128, NT), np.float32)
        ic = np.zeros((128, 4, 16), np.float32)
        if r == 0:
            xin[2048:] = x[b, :2048]
            kb[:, :16] = -30000.0
            for g in range(4):
                w = 2 << g
                ic[:, g, :] = 1.0 / np.minimum(np.arange(16) + 1.0, float(w))[None, :]
        else:
            xin[:] = x[b]
            for g in range(4):
                ic[:, g, :] = 1.0 / float(2 << g)
        m = {"xin": xin, "kbias": kb, "icnt": ic}
        m.update(shared)
        maps.append(m)
    return maps


_NC_CACHE = {}


def kernel(**inputs):
    if "nc" not in _NC_CACHE:
        _NC_CACHE["nc"] = build_program()
    nc = _NC_CACHE["nc"]
    maps = make_in_maps(inputs)
    res = run_bass_kernel_spmd(nc, maps, core_ids=list(range(8)))
    outp = np.zeros((4, 4096, D), np.float32)
    for c in range(8):
        b, r = c // 2, c % 2
        outp[b, r * 2048:(r + 1) * 2048] = res.results[c]["out"]
    return outp
```

```python
import contextlib
import math
import numpy as np
import concourse.bass as bass
import concourse.mybir as mybir
from concourse.bass_utils import run_bass_kernel_spmd

F32 = mybir.dt.float32
BF16 = mybir.dt.bfloat16
I32 = mybir.dt.int32
AF = mybir.ActivationFunctionType
ALU = mybir.AluOpType
AX = mybir.AxisListType

ENGS = ("tensor", "vector", "scalar", "gpsimd", "sync")
N_DSEM = 64

NT, T0, T1 = 32, 15, 16
N0 = NT - T0
N1 = NT - T1
D = 1024
EPS = 1e-6
TWO_PI = 2.0 * math.pi
MASKV = -240000.0


class Buf:
    __slots__ = ("name", "writers", "readers")

    def __init__(self, name=""):
        self.name = name
        self.writers = {}
        self.readers = {}


class Plan:
    def __init__(self, nc, sems, dsems):
        self.nc, self.sems, self.dsems = nc, sems, dsems
        self.ops = {e: [] for e in ENGS}
        self.cnt = {e: 0 for e in ENGS}
        self.seen = {e: {} for e in ENGS}
        self.needed = set()
        self.dval = [0] * N_DSEM
        self.dnext = 0
        self.dnx = {}
        self.flushed = {e: 0 for e in ENGS}
        self.marks = {e: [] for e in ENGS}
        self.markval = {}
        self.markcnt = {e: 0 for e in ENGS}
        self.last_real = {e: 0 for e in ENGS}

    def _need(self, eng, prod, val, waits):
        if prod == eng and eng == "tensor":
            return
        if not isinstance(prod, tuple) and val <= self.flushed[prod]:
            for m in self.marks[prod]:
                if m >= val:
                    val = m
                    break
        if self.seen[eng].get(prod, 0) >= val:
            return
        self.seen[eng][prod] = val
        waits.append((prod, val))
        if not isinstance(prod, tuple):
            self.needed.add((prod, val))

    def _deps(self, eng, me, reads, writes, waits):
        for b in reads:
            for p, v in b.writers.items():
                if p != me:
                    self._need(eng, p, v, waits)
                elif eng != "tensor" and not isinstance(me, tuple):
                    self._need(eng, p, v, waits)
        for b in writes:
            for p, v in list(b.writers.items()):
                if p == me:
                    continue
                if isinstance(p, tuple) and isinstance(me, tuple):
                    continue
                self._need(eng, p, v, waits)
            for p, v in b.readers.items():
                if p != me:
                    self._need(eng, p, v, waits)

    def _commit(self, me, val, reads, writes):
        for b in reads:
            b.readers[me] = val
        for b in writes:
            if isinstance(me, tuple):
                b.writers = {p: v for p, v in b.writers.items() if isinstance(p, tuple)}
            else:
                b.writers = {}
            b.writers[me] = val
            b.readers = {}

    def op(self, eng, fn, reads=(), writes=()):
        waits = []
        self._deps(eng, eng, reads, writes, waits)
        self.cnt[eng] += 1
        idx = self.cnt[eng]
        self.ops[eng].append([waits, fn, idx, None])
        self.last_real[eng] = idx
        self._commit(eng, idx, reads, writes)

    def dma(self, eng, fn, reads=(), writes=()):
        lo, hi = (0, 24) if eng == "sync" else (24, N_DSEM)
        k = self.dnx.get(eng, lo)
        self.dnx[eng] = lo + (k + 1 - lo) % (hi - lo)
        me = ("d", k)
        waits = []
        if self.dval[k] > 0:
            self._need(eng, me, self.dval[k], waits)
        self._deps(eng, me, reads, writes, waits)
        self.dval[k] += 16
        self.cnt[eng] += 1
        self.ops[eng].append([waits, fn, self.cnt[eng], k])
        self._commit(me, self.dval[k], reads, writes)

    def barrier(self):
        for e in ENGS:
            waits = []
            for p in ENGS:
                if p != e and self.last_real[p] > 0:
                    self._need(e, p, self.last_real[p], waits)
            for k in range(N_DSEM):
                if self.dval[k] > 0:
                    self._need(e, ("d", k), self.dval[k], waits)
            self.cnt[e] += 1
            self.ops[e].append([waits, None, self.cnt[e], None])

    def flush(self):
        nc, sems, dsems = self.nc, self.sems, self.dsems
        for e in ENGS:
            real = [o for o in self.ops[e] if o[1] is not None and o[3] is None]
            if real:
                self.needed.add((e, real[-1][2]))
            for (_, fn, idx, dk) in self.ops[e]:
                if (e, idx) in self.needed and fn is not None and dk is None:
                    self.markcnt[e] += 1
                    self.markval[(e, idx)] = self.markcnt[e]
                    self.marks[e].append(idx)
        plan = self

        def run(e, engine):
            for waits, fn, idx, dk in plan.ops[e]:
                for prod, val in waits:
                    if isinstance(prod, tuple):
                        engine.wait_ge(dsems[prod[1]], val)
                    else:
                        engine.wait_ge(sems[prod], plan.markval[(prod, val)])
                if fn is None:
                    continue
                ins = fn(engine)
                if dk is not None:
                    ins.then_inc(dsems[dk], 16)
                elif (e, idx) in plan.markval:
                    ins.then_inc(sems[e], 1)

        with nc.Block() as block:
            @block.tensor
            def _(eng):
                run("tensor", eng)

            @block.vector
            def _(eng):
                run("vector", eng)

            @block.scalar
            def _(eng):
                run("scalar", eng)

            @block.gpsimd
            def _(eng):
                run("gpsimd", eng)

            @block.sync
            def _(eng):
                run("sync", eng)
        for e in ENGS:
            self.flushed[e] = self.cnt[e]
            self.ops[e] = []


class K:
    def __init__(self, P):
        self.P = P
        self.rec = None

    def _do(self, fn):
        if self.rec is not None:
            self.rec.append(fn)
        else:
            fn()

    def record(self, f):
        assert self.rec is None
        self.rec = []
        try:
            f()
            return self.rec
        finally:
            self.rec = None

    def mm(self, out, lhsT, rhs, start, stop, R, W, **kw):
        self._do(lambda: self.P.op("tensor", lambda e: e.matmul(out, lhsT=lhsT, rhs=rhs, start=start, stop=stop, **kw), R, W))

    def tr(self, out, in_, ident, R, W):
        self._do(lambda: self.P.op("tensor", lambda e: e.transpose(out, in_, ident), R, W))

    def v(self, name, *a, R=(), W=(), eng="vector", **kw):
        self._do(lambda: self.P.op(eng, lambda e: getattr(e, name)(*a, **kw), R, W))

    def act(self, out, in_, func, R, W, **kw):
        self._do(lambda: self.P.op("scalar", lambda e: e.activation(out, in_, func, **kw), R, W))

    def dma(self, eng, out, in_, R, W, **kw):
        self._do(lambda: self.P.dma(eng, lambda e: e.dma_start(out=out, in_=in_, **kw), R, W))


def bcast(ap, axis, n):
    a = ap.unsqueeze(axis)
    shp = list(a.shape)
    shp[axis] = n
    return a.broadcast_to(shp)


def build_program(dbg=None):
    nc = bass.Bass("TRN2", target_bir_lowering=False)

    def din(name, shape):
        return nc.dram_tensor(name, list(shape), F32, kind="ExternalInput").ap()

    def dscr(name, shape, dt):
        return nc.dram_tensor(name, list(shape), dt, kind="Internal").ap()

    xin = din("xin", [NT * 128, D])
    kbias = din("kbias", [128, NT])
    icnt = din("icnt", [128, 4, 16])
    g_mixpre = din("mix_pre_g", [2, D]); g_mixpost = din("mix_post_g", [2, D])
    g_mlppre = din("mlp_pre_g", [2, D]); g_mlppost = din("mlp_post_g", [2, D])
    w_in_e = din("w_in_even", [1, D, 2056])
    lam_re = din("s5_lam_re", [1, 32, 64]); lam_im = din("s5_lam_im", [1, 32, 64])
    log_dt = din("s5_log_dt", [1, 32])
    b_re = din("s5_b_re", [1, 32, 64, 16]); b_im = din("s5_b_im", [1, 32, 64, 16])
    c_re = din("s5_c_re", [1, 32, 16, 64]); c_im = din("s5_c_im", [1, 32, 16, 64])
    s5_d = din("s5_d", [1, 512]); w_glu = din("s5_w_glu", [1, 512, 512])
    fox_bf = din("fox_b_f", [1, 8])
    w_out_e = din("w_out_even", [1, D, D])
    w_in_o = din("w_in_odd", [1, D, 1536])
    pool_w = din("pool_w", [1, 4, 128, 128]); pool_scale = din("pool_scale", [1, 512])
    ln_g = din("sgu_ln_g", [1, 512]); ln_b = din("sgu_ln_b", [1, 512])
    w_s = din("sgu_w_s", [1, 4, 128, 128]); b_s = din("sgu_b_s", [1, 4, 128])
    w_out_o = din("w_out_odd", [1, D, D])
    mlp_w1 = din("mlp_w1", [2, D, 4096]); mlp_w2 = din("mlp_w2", [2, 4096, D])
    out = nc.dram_tensor("out", [N1 * 128, D], F32, kind="ExternalOutput").ap()

    UT_d = dscr("UT_d", [4, 128, NT * 128], BF16)
    KT_d = dscr("KT_d", [8, 64, NT * 128], BF16)
    QT_d = dscr("QT_d", [8, 67, N0 * 128], BF16)
    V_d = dscr("V_d", [NT, 128, 544], BF16)
    YC_d = dscr("YC_d", [8, 128, N0 * 128], BF16)
    X1_d = dscr("X1_d", [N0 * 128, D], F32)
    X2_d = dscr("X2_d", [N0 * 128, D], F32)
    YC1_d = dscr("YC1_d", [8, 128, N1 * 128], BF16)
    X3_d = dscr("X3_d", [N1 * 128, D], F32)
    W1b = [dscr(f"W1b{l}", [D, 4096], BF16) for l in range(2)]
    W2b = [dscr(f"W2b{l}", [4096, D], BF16) for l in range(2)]
    WOb = [dscr(f"WOb{l}", [D, D], BF16) for l in range(2)]
    WIOb = dscr("WIOb", [D, 1536], BF16)
    B_W1b = [Buf(), Buf()]; B_W2b = [Buf(), Buf()]; B_WOb = [Buf(), Buf()]; B_WIOb = Buf()
    dbg_o = None
    if dbg is not None:
        dbg_o = nc.dram_tensor("dbg", list(dbg), F32, kind="ExternalOutput").ap()

    B_UT, B_KT, B_QT, B_V, B_YC, B_X1, B_X2, B_YC1, B_X3, B_OUT = (Buf(n) for n in
        ["UT", "KT", "QT", "V", "YC", "X1", "X2", "YC1", "X3", "OUT"])

    es0 = contextlib.ExitStack()
    with es0:
        sems = {e: es0.enter_context(nc.semaphore("s_" + e)) for e in ENGS}
        dsems = [es0.enter_context(nc.semaphore(f"d{i}")) for i in range(N_DSEM)]
        P = Plan(nc, sems, dsems)
        k = K(P)

        def sb(es, name, shape, dt):
            return es.enter_context(nc.sbuf_tensor(name, list(shape), dt))

        def pst(es, name, shape, dt=F32):
            return es.enter_context(nc.psum_tensor(name, list(shape), dt))

        identb = sb(es0, "identb", [128, 128], BF16); identf = sb(es0, "identf", [128, 128], F32)
        triu = sb(es0, "triu", [128, 128], F32); onesf = sb(es0, "onesf", [128, 128], F32)
        maskb = sb(es0, "maskb", [128, 128], BF16); tril = sb(es0, "tril", [128, 128], F32)
        maskE = sb(es0, "maskE", [128, 2], F32); maskP = sb(es0, "maskP", [128, 4], F32)
        Fkb = sb(es0, "Fkb", [128, NT, 8], F32)
        kb_sb = sb(es0, "kb_sb", [128, NT], F32)
        B_const = Buf("const"); B_Fkb = Buf("Fkb")
        with contextlib.ExitStack() as es:
            dI = sb(es, "dI", [128, 128], I32); dF = sb(es, "dF", [128, 128], F32)
            pI = sb(es, "pI", [128, 8], I32); pF = sb(es, "pF", [128, 8], F32); ge = sb(es, "ge", [128, 8], F32)
            cI = sb(es, "cI", [128, 8], I32); cF = sb(es, "cF", [128, 8], F32)
            Bt = Buf("t")
            k.v("iota", dI[:], [[1, 128]], base=0, channel_multiplier=-1, eng="gpsimd", W=[Bt])
            k.v("tensor_copy", dF[:], dI[:], R=[Bt], W=[Bt])
            k.v("tensor_single_scalar", identf[:], dF[:], 0.0, ALU.is_equal, R=[Bt], W=[B_const])
            k.v("tensor_copy", identb[:], identf[:], R=[B_const], W=[B_const])
            k.v("tensor_single_scalar", triu[:], dF[:], 0.0, ALU.is_ge, R=[Bt], W=[B_const])
            k.v("tensor_single_scalar", tril[:], dF[:], 0.0, ALU.is_le, R=[Bt], W=[B_const])
            k.v("tensor_scalar", maskb[:], dF[:], 0.0, MASKV, ALU.is_lt, ALU.mult, R=[Bt], W=[B_const])
            k.v("memset", onesf[:], 1.0, W=[B_const])
            k.v("iota", pI[:], [[0, 8]], base=0, channel_multiplier=1, eng="gpsimd", W=[Bt])
            k.v("iota", cI[:], [[32, 8]], base=0, channel_multiplier=0, eng="gpsimd", W=[Bt])
            k.v("tensor_copy", pF[:], pI[:], R=[Bt], W=[Bt])
            k.v("tensor_copy", cF[:], cI[:], R=[Bt], W=[Bt])
            k.v("tensor_tensor", ge[:], pF[:], cF[:], ALU.is_ge, R=[Bt], W=[Bt])
            k.v("tensor_tensor", maskP[:], ge[:, 0:4], ge[:, 1:5], ALU.subtract, R=[Bt], W=[B_const])
            k.v("tensor_copy", maskE[:, 1:2], ge[:, 2:3], R=[Bt], W=[B_const])
            k.v("tensor_tensor", maskE[:, 0:1], ge[:, 0:1], ge[:, 2:3], ALU.subtract, R=[Bt], W=[B_const])
            k.dma("sync", kb_sb[:], kbias, [], [B_const])
            P.barrier(); P.flush()

        def load_w_bf16(wt, wbuf, src, nkt, ncols, col0=0, defer=None, per=8192, cw=2048, srcbuf=None):
            for c0 in range(0, ncols, cw):
                c1 = min(ncols, c0 + cw)
                step = max(1, per // (c1 - c0))
                for k0 in range(0, nkt, step):
                    k1 = min(nkt, k0 + step)

                    def thunk(k0=k0, k1=k1, c0=c0, c1=c1):
                        k.dma("gpsimd", wt[:, k0:k1, c0:c1],
                              src[k0 * 128:k1 * 128, col0 + c0:col0 + c1].rearrange("(k p) c -> p k c", p=128),
                              [srcbuf] if srcbuf is not None else [], [wbuf])
                    if defer is None:
                        thunk()
                    else:
                        defer.append(thunk)

        conv_pend = []

        def convert_w(dst, bdst, src, rows_per, ncols):
            nrows = src.shape[0]
            for c0 in range(0, ncols, 2048):
                c1 = min(ncols, c0 + 2048)
                for r0 in range(0, nrows, rows_per):
                    conv_pend.append(lambda r0=r0, c0=c0, c1=c1: k.dma(
                        "gpsimd", dst[r0:r0 + rows_per, c0:c1], src[r0:r0 + rows_per, c0:c1], [], [bdst]))

        def convert_all_weights():
            convert_w(WOb[0], B_WOb[0], w_out_e[0], 512, D)
            convert_w(W1b[0], B_W1b[0], mlp_w1[0], 256, 4096)
            convert_w(W2b[0], B_W2b[0], mlp_w2[0], 512, D)
            n0 = len(conv_pend)
            convert_w(WIOb, B_WIOb, w_in_o[0], 512, 1536)
            convert_w(WOb[1], B_WOb[1], w_out_o[0], 512, D)
            convert_w(W1b[1], B_W1b[1], mlp_w1[1], 256, 4096)
            convert_w(W2b[1], B_W2b[1], mlp_w2[1], 512, D)
            conv_pend1.extend(conv_pend[n0:])
            del conv_pend[n0:]

        conv_pend1 = []

        def load_bcast_row(t, tb, src_row, n):
            k.dma("sync", t[:, 0:n], bcast(src_row, 0, 128), [], [tb])

        def rms_rstd(ss_ap, rstd, tmp, Bss, Brs, n):
            k.v("tensor_scalar", tmp, ss_ap, 1.0 / n, EPS, ALU.mult, ALU.add, R=[Bss], W=[Brs])
            k.act(tmp, tmp, AF.Sqrt, [Brs], [Brs])
            k.v("reciprocal", rstd, tmp, R=[Brs], W=[Brs])

        def norm_in(xt, Bx, gt, Bg, hb, Bh, junk, Bj, st, Bst):
            k.act(hb[:], xt[:], AF.Square, [Bx], [Bh, Bst], accum_out=st[:, 0:1])
            rms_rstd(st[:, 0:1], st[:, 2:3], st[:, 1:2], Bst, Bst, D)
            k.v("scalar_tensor_tensor", hb[:], xt[:], st[:, 2:3], gt[:, 0:D], ALU.mult, ALU.mult,
                R=[Bx, Bst, Bg], W=[Bh])

        def to_fm(hb, Bh, tp, Btp, hT, BhT, col0, use_act):
            for kt in range(8):
                k.tr(tp[:, kt * 128:(kt + 1) * 128], hb[:, kt * 128:(kt + 1) * 128], identb[:], [Bh, B_const], [Btp])
            src = tp[:].rearrange("p (k t) -> p k t", k=8)
            if use_act:
                k.act(hT[:, :, col0:col0 + 128], src, AF.Copy, [Btp], [BhT])
            else:
                k.v("tensor_copy", hT[:, :, col0:col0 + 128], src, R=[Btp], W=[BhT])

        pn_ctr = [0]

        def post_norm_res(o_ps, Bo, gt, Bg, xt, Bx, res, Br, junk, Bj, st, Bst, add_eng=None):
            k.act(res[:], o_ps, AF.Square, [Bo], [Br, Bst], accum_out=st[:, 0:1])
            rms_rstd(st[:, 0:1], st[:, 2:3], st[:, 1:2], Bst, Bst, D)
            k.v("scalar_tensor_tensor", res[:], o_ps, st[:, 2:3], gt[:, 0:D], ALU.mult, ALU.mult,
                R=[Bo, Bst, Bg], W=[Br])
            pn_ctr[0] += 1
            k.v("tensor_tensor", res[:], res[:], xt[:], ALU.add, R=[Br, Bx], W=[Br],
                eng=(add_eng or ("gpsimd" if pn_ctr[0] % 2 == 0 else "vector")))

        def phase_outproj_mlp(tag, YCd, B_YCd, ntile, w_out_l, g_row, xsrc, B_xsrc, xmid, B_xmid,
                              xdst, B_xdst, w1_l, w2_l, g_pre_row, g_post_row, Bwo_s, Bw1_s, Bw2_s):
            GT = 4
            with contextlib.ExitStack() as eo:
                w1 = sb(eo, "w1" + tag, [128, 8, 4096], BF16); Bw1 = Buf()
                w2 = sb(eo, "w2" + tag, [128, 32, D], BF16); Bw2 = Buf()
                gp = sb(eo, "g1" + tag, [128, D], F32); gq = sb(eo, "g2" + tag, [128, D], F32); Bgm = Buf()
                with contextlib.ExitStack() as es:
                    wo = sb(es, "wo" + tag, [128, 8, D], BF16); Bwo = Buf()
                    gt = sb(es, "go" + tag, [128, D], F32); Bg = Buf()
                    load_w_bf16(wo, Bwo, w_out_l, 8, D, srcbuf=Bwo_s)
                    load_bcast_row(gt, Bg, g_row, D)
                    pend = []
                    load_w_bf16(w1, Bw1, w1_l, 8, 4096, defer=pend, per=2048, cw=2048, srcbuf=Bw1_s)
                    load_w_bf16(w2, Bw2, w2_l, 32, D, defer=pend, per=2048, cw=1024, srcbuf=Bw2_s)
                    NB = 4
                    NP = 3
                    yc = [sb(es, f"yc{tag}{i}", [128, 8, 128], BF16) for i in range(NB)]
                    xt = [sb(es, f"xo{tag}{i}", [128, D], F32) for i in range(NB)]
                    rs = [sb(es, f"ro{tag}{i}", [128, D], F32) for i in range(NB)]
                    jk = sb(es, "jo" + tag, [128, D], BF16); Bj = Buf()
                    st = [sb(es, f"so{tag}{i}", [128, 4], F32) for i in range(NB)]
                    ops = [pst(es, f"po{tag}{i}", [128, D]) for i in range(NP)]
                    Byc = [Buf() for _ in range(NB)]; Bxt = [Buf() for _ in range(NB)]; Brs = [Buf() for _ in range(NB)]
                    Bst = [Buf() for _ in range(NB)]; Bop = [Buf() for _ in range(NP)]

                    def ld(t):
                        i = t % NB
                        k.dma("gpsimd", yc[i][:], YCd[:, :, t * 128:(t + 1) * 128].rearrange("f p t -> p f t"), [B_YCd], [Byc[i]])
                        k.dma("gpsimd", xt[i][:], xsrc[t * 128:(t + 1) * 128, :], [B_xsrc], [Bxt[i]])

                    ld(0)
                    if ntile > 1:
                        ld(1)
                    for t in range(ntile):
                        i = t % NB
                        pi = t % NP
                        if t + 2 < ntile:
                            ld(t + 2)
                        for _ in range(2):
                            if pend:
                                pend.pop(0)()
                        for hf in range(2):
                            for ft in range(8):
                                k.mm(ops[pi][:, hf * 512:(hf + 1) * 512], yc[i][:, ft, :], wo[:, ft, hf * 512:(hf + 1) * 512],
                                     ft == 0, ft == 7, [Byc[i], Bwo], [Bop[pi]])
                        post_norm_res(ops[pi][:], Bop[pi], gt, Bg, xt[i], Bxt[i], rs[i], Brs[i], jk, Bj, st[i], Bst[i],
                                      add_eng="vector")
                        k.dma("sync", xmid[t * 128:(t + 1) * 128, :], rs[i][:], [Brs[i]], [B_xmid])
                    while pend:
                        pend.pop(0)()
                    load_bcast_row(gp, Bgm, g_pre_row, D)
                    load_bcast_row(gq, Bgm, g_post_row, D)
                    P.barrier(); P.flush()
                with contextlib.ExitStack() as es:
                    Bg = Bgm
                    NX = GT + 1
                    xt = [sb(es, f"xm{tag}{i}", [128, D], F32) for i in range(NX)]; Bxt = [Buf() for _ in range(NX)]
                    hb = sb(es, "hbm" + tag, [128, D], BF16); Bhb = Buf()
                    st = sb(es, "sm" + tag, [128, 4], F32); Bst = Buf()
                    st2 = sb(es, "sm2" + tag, [128, 4], F32); Bst2 = Buf()
                    hT = sb(es, "hTm" + tag, [128, 8, GT * 128], BF16); BhT = Buf()
                    A = sb(es, "Am" + tag, [128, 32, GT * 128], BF16); BA = [Buf() for _ in range(32)]
                    rl = [sb(es, f"rl{tag}{i}", [128, GT * 128], BF16) for i in range(2)]; Brl = [Buf(), Buf()]
                    rs1 = sb(es, f"rsm{tag}", [128, D], F32); rs = [rs1, rs1]; Brs1 = Buf(); Brs = [Brs1, Brs1]
                    tp = pst(es, "tpm" + tag, [128, 1024], BF16); Btp = Buf()
                    hp = [pst(es, f"hp{tag}{i}", [128, 512]) for i in range(3)]; Bhp = [Buf() for _ in range(3)]
                    op_ = [pst(es, f"opm{tag}{i}", [128, D]) for i in range(2)]; Bop = [Buf(), Buf()]
                    groups_ = [(g0, min(GT, ntile - g0)) for g0 in range(0, ntile, GT)]

                    def xload(t):
                        k.dma("sync", xt[t % NX][:], xmid[t * 128:(t + 1) * 128, :], [B_xmid], [Bxt[t % NX]])

                    def prep(t, j):
                        norm_in(xt[t % NX], Bxt[t % NX], gp, Bg, hb, Bhb, None, None, st, Bst)
                        to_fm(hb, Bhb, tp, Btp, hT, BhT, j * 128, j % 2 == 1)

                    for j in range(groups_[0][1]):
                        xload(j)
                    for j in range(groups_[0][1]):
                        prep(j, j)
                    for gi_, (g0, nt) in enumerate(groups_):
                        n = nt * 128
                        nxt = groups_[gi_ + 1] if gi_ + 1 < len(groups_) else None
                        if nxt is not None:
                            xload(nxt[0])
                        for f in range(32):
                            pi = f % 3
                            if tag == "0" and conv_pend1 and f % 4 == 1:
                                conv_pend1.pop(0)()
                            for kt in range(8):
                                k.mm(hp[pi][:, 0:n], w1[:, kt, f * 128:(f + 1) * 128], hT[:, kt, 0:n], kt == 0, kt == 7,
                                     [Bw1, BhT], [Bhp[pi]])
                            ri = f % 2
                            k.act(rl[ri][:, 0:n], hp[pi][:, 0:n], AF.Relu, [Bhp[pi]], [Brl[ri]])
                            k.v("tensor_tensor", A[:, f, 0:n], rl[ri][:, 0:n], rl[ri][:, 0:n], ALU.mult,
                                R=[Brl[ri]], W=[BA[f]], eng=("vector" if f % 2 == 0 else "gpsimd"))
                        for j in range(nt):
                            t = g0 + j
                            oi = t % 2
                            for hf in range(2):
                                for f in range(32):
                                    k.mm(op_[oi][:, hf * 512:(hf + 1) * 512], A[:, f, j * 128:(j + 1) * 128],
                                         w2[:, f, hf * 512:(hf + 1) * 512], f == 0, f == 31, [BA[f], Bw2], [Bop[oi]])
                            if nxt is not None and j < nxt[1]:
                                prep(nxt[0] + j, j)
                            post_norm_res(op_[oi][:], Bop[oi], gq, Bg, xt[t % NX], Bxt[t % NX], rs[oi], Brs[oi], None, None,
                                          st2, Bst2)
                            k.dma("sync", xdst[t * 128:(t + 1) * 128, :], rs[oi][:], [Brs[oi]], [B_xdst])
                            if nxt is not None and j + 1 < nxt[1]:
                                xload(nxt[0] + j + 1)
                    if tag == "0":
                        while conv_pend1:
                            conv_pend1.pop(0)()
                    P.barrier(); P.flush()

        WE = w_in_e[0]
        with contextlib.ExitStack() as es:
            GT = 4
            NG = NT // GT
            wi = sb(es, "wiA", [128, 8, 2056], BF16); Bwi = Buf()
            gp = sb(es, "gA", [128, D], F32); Bg = Buf()
            bfb = sb(es, "bfb", [128, 8], F32)
            load_bcast_row(gp, Bg, g_mixpre[0], D)
            load_bcast_row(bfb, Bg, fox_bf[0], 8)
            load_w_bf16(wi, Bwi, WE, 8, 2056, per=4096, cw=512)
            xt = [[sb(es, f"xA{i}_{j}", [128, D], F32) for j in range(GT)] for i in range(2)]
            Bxt = [[Buf() for j in range(GT)] for i in range(2)]
            hb = [sb(es, f"hbA{j}", [128, D], BF16) for j in range(GT)]; Bhb = [Buf() for _ in range(GT)]
            jk = sb(es, "jA", [128, D], BF16); Bj = Buf()
            st = sb(es, "sA", [128, 3, GT], F32); Bst = Buf()
            hT = [sb(es, f"hTA{i}", [128, 8, GT * 128], BF16) for i in range(2)]; BhT = [Buf(), Buf()]
            us = [sb(es, f"usA{i}", [128, 4, 512], BF16) for i in range(2)]; Bus = [Buf(), Buf()]
            ks = [sb(es, f"ksA{i}", [128, 4, 512], BF16) for i in range(2)]; Bks = [Buf(), Buf()]
            q2 = [sb(es, f"q2A{i}", [128, 4, 512], BF16) for i in range(2)]; Bq2 = [Buf(), Buf()]
            qs = [sb(es, f"qsA{i}", [67, 8, 512], BF16) for i in range(2)]; Bqs = [Buf(), Buf()]
            vs = [sb(es, f"vsA{i}", [128, 8, 68], BF16) for i in range(4)]; Bvs = [Buf() for _ in range(4)]
            carry = sb(es, "carry", [128, 8], F32); Bcar = Buf()
            fx = sb(es, "fxA", [128, 8, GT, 8], F32); Bfx = Buf()
            fsp = sb(es, "fsp", [128, GT, 8, 3], BF16); Bfsp = Buf()
            tp = [pst(es, f"tpA{i}", [128, 1024], BF16) for i in range(2)]; Btp = [Buf(), Buf()]
            pp = [pst(es, f"ppA{i}", [128, 512]) for i in range(4)]; Bpp = [Buf() for _ in range(4)]
            pf = pst(es, "pfA", [128, 512]); Bpf = Buf()
            pa = pst(es, "paA", [128, 512]); Bpa = Buf()
            k.v("memset", carry[:], 0.0, W=[Bcar])
            for i in range(4):
                k.v("memset", vs[i][:, :, 64:68], 1.0, W=[Bvs[i]])
            ctr = {"pp": 0, "tp": 0}

            def nextpp():
                i = ctr["pp"] % 4; ctr["pp"] += 1
                return pp[i], Bpp[i]

            def s1_load(g):
                for j in range(GT):
                    t = g * GT + j
                    k.dma("sync", xt[g % 2][j][:], xin[t * 128:(t + 1) * 128, :], [], [Bxt[g % 2][j]])

            def s1_norm(g):
                X, BX = xt[g % 2], Bxt[g % 2]
                for j in range(GT):
                    k.act(hb[j][:], X[j][:], AF.Square, [BX[j]], [Bhb[j], Bst], accum_out=st[:, 0, j:j + 1])
                rms_rstd(st[:, 0, :], st[:, 2, :], st[:, 1, :], Bst, Bst, D)
                for j in range(GT):
                    k.v("scalar_tensor_tensor", hb[j][:], X[j][:], st[:, 2, j:j + 1], gp[:, 0:D], ALU.mult, ALU.mult,
                        R=[BX[j], Bst, Bg], W=[Bhb[j]])

            def s1_tr(g):
                for j in range(GT):
                    ti = ctr["tp"] % 2; ctr["tp"] += 1
                    to_fm(hb[j], Bhb[j], tp[ti], Btp[ti], hT[g % 2], BhT[g % 2], j * 128, j % 2 == 1)

            def s2_uk(g):
                gi = g % 2
                H, BH = hT[gi], BhT[gi]
                for ft in range(4):
                    p_, Bp_ = nextpp()
                    for kt in range(8):
                        k.mm(p_[:, :], wi[:, kt, ft * 128:(ft + 1) * 128], H[:, kt, :], kt == 0, kt == 7, [Bwi, BH], [Bp_])
                    if ft % 2 == 0:
                        k.v("tensor_copy", us[gi][:, ft, :], p_[:, :], R=[Bp_], W=[Bus[gi]])
                    else:
                        k.act(us[gi][:, ft, :], p_[:, :], AF.Copy, [Bp_], [Bus[gi]])
                k.dma("gpsimd", UT_d[:, :, g * 512:(g + 1) * 512].rearrange("f p t -> p f t"), us[gi][:], [Bus[gi]], [B_UT])
                for hp in range(4):
                    p_, Bp_ = nextpp()
                    c0 = 1024 + hp * 128
                    for kt in range(8):
                        k.mm(p_[:, :], wi[:, kt, c0:c0 + 128], H[:, kt, :], kt == 0, kt == 7, [Bwi, BH], [Bp_])
                    if hp % 2 == 0:
                        k.v("tensor_copy", ks[gi][:, hp, :], p_[:, :], R=[Bp_], W=[Bks[gi]])
                    else:
                        k.act(ks[gi][:, hp, :], p_[:, :], AF.Copy, [Bp_], [Bks[gi]])
                k.dma("gpsimd", KT_d[:, :, g * 512:(g + 1) * 512].rearrange("(hp two) p t -> (two p) hp t", two=2),
                      ks[gi][:], [Bks[gi]], [B_KT])

            def s2_q(g):
                gi = g % 2
                H, BH = hT[gi], BhT[gi]
                for hp in range(4):
                    p_, Bp_ = nextpp()
                    c0 = 512 + hp * 128
                    for kt in range(8):
                        k.mm(p_[:, :], wi[:, kt, c0:c0 + 128], H[:, kt, :], kt == 0, kt == 7, [Bwi, BH], [Bp_])
                    if hp % 2 == 1:
                        k.v("tensor_copy", q2[gi][:, hp, :], p_[:, :], R=[Bp_], W=[Bq2[gi]])
                    else:
                        k.act(q2[gi][:, hp, :], p_[:, :], AF.Copy, [Bp_], [Bq2[gi]])
                j0 = max(0, T0 - g * GT)
                cq0, cq1 = (g * GT + j0 - T0) * 128, ((g + 1) * GT - T0) * 128
                for two in range(2):
                    k.dma("gpsimd", QT_d[two::2, 0:64, cq0:cq1].rearrange("hp p t -> p hp t"),
                          q2[gi][two * 64:(two + 1) * 64, :, j0 * 128:GT * 128], [Bq2[gi]], [B_QT])

            def s2_vf(g, need_q):
                gi = g % 2
                H, BH = hT[gi], BhT[gi]
                for j in range(GT):
                    t = g * GT + j
                    p_, Bp_ = nextpp()
                    for kt in range(8):
                        k.mm(p_[:, :], H[:, kt, j * 128:(j + 1) * 128], wi[:, kt, 1536:2048], kt == 0, kt == 7, [Bwi, BH], [Bp_])
                    if j % 2 == 0:
                        k.v("tensor_copy", vs[j][:, :, 0:64], p_[:, :].rearrange("p (h d) -> p h d", h=8), R=[Bp_], W=[Bvs[j]])
                    else:
                        k.act(vs[j][:, :, 0:64], p_[:, :].rearrange("p (h d) -> p h d", h=8), AF.Copy, [Bp_], [Bvs[j]])
                    k.dma("gpsimd", V_d[t], vs[j][:].rearrange("p h d -> p (h d)"), [Bvs[j]], [B_V])
                for j in range(GT):
                    for kt in range(8):
                        k.mm(pf[:, j * 8:(j + 1) * 8], H[:, kt, j * 128:(j + 1) * 128], wi[:, kt, 2048:2056], kt == 0, kt == 7,
                             [Bwi, BH], [Bpf])
                k.v("tensor_tensor", fx[:, 0, :, :], pf[:, 0:32].rearrange("p (j h) -> p j h", j=GT), bcast(bfb[:], 1, GT),
                    ALU.add, R=[Bpf, Bg], W=[Bfx])
                k.act(fx[:, 1, :, :], fx[:, 0, :, :], AF.Exp, [Bfx], [Bfx], scale=-1.0)
                k.act(fx[:, 2, :, :], fx[:, 1, :, :], AF.Ln, [Bfx], [Bfx], bias=1.0)
                sp = fx[:, 2, :, :].rearrange("p j h -> p (j h)")
                k.mm(pf[:, 32:64], triu[:], sp, True, True, [Bfx, B_const], [Bpf])
                k.mm(pf[:, 64:96], onesf[:], sp, True, True, [Bfx, B_const], [Bpf])
                for j in range(GT):
                    t = g * GT + j
                    k.v("tensor_tensor", fx[:, 3, j, :], pf[:, 32 + j * 8:40 + j * 8], carry[:], ALU.add, R=[Bpf, Bcar], W=[Bfx])
                    k.v("tensor_scalar", Fkb[:, t, :], fx[:, 3, j, :], kb_sb[:, t:t + 1], None, ALU.add, R=[Bfx, B_const], W=[B_Fkb])
                    k.v("tensor_tensor", carry[:], pf[:, 64 + j * 8:72 + j * 8], carry[:], ALU.add, R=[Bpf, Bcar], W=[Bcar])
                if need_q:
                    k.v("tensor_scalar", fx[:, 4, :, :], fx[:, 3, :, :], -8.0, None, ALU.mult, R=[Bfx], W=[Bfx])
                    k.v("tensor_copy", fsp[:, :, :, 0], fx[:, 4, :, :], R=[Bfx], W=[Bfsp])
                    k.v("tensor_tensor", fx[:, 5, :, :], fx[:, 4, :, :], fsp[:, :, :, 0], ALU.subtract, R=[Bfx, Bfsp], W=[Bfx])
                    k.v("tensor_copy", fsp[:, :, :, 1], fx[:, 5, :, :], R=[Bfx], W=[Bfsp])
                    k.v("tensor_tensor", fx[:, 6, :, :], fx[:, 5, :, :], fsp[:, :, :, 1], ALU.subtract, R=[Bfx, Bfsp], W=[Bfx])
                    k.v("tensor_copy", fsp[:, :, :, 2], fx[:, 6, :, :], R=[Bfx], W=[Bfsp])
                    for hh in range(2):
                        for j in range(GT):
                            for h4 in range(4):
                                h = hh * 4 + h4
                                k.mm(pa[64:67, h4 * 128:(h4 + 1) * 128], fsp[:, j, h, :], identb[:], True, True,
                                     [Bfsp, B_const], [Bpa], tile_position=(0, 64))
                            k.v("tensor_copy", qs[gi][64:67, hh * 4:hh * 4 + 4, j * 128:(j + 1) * 128],
                                pa[64:67, :].rearrange("p (h t) -> p h t", h=4), R=[Bpa], W=[Bqs[gi]])
                    j0 = max(0, T0 - g * GT)
                    k.dma("gpsimd", QT_d[:, 64:67, (g * GT + j0 - T0) * 128:((g + 1) * GT - T0) * 128].rearrange("h p t -> p h t"),
                          qs[gi][64:67, :, j0 * 128:GT * 128], [Bqs[gi]], [B_QT])

            s1_load(0); s1_norm(0); s1_tr(0)
            for g in range(NG):
                need_q = (g * GT + GT - 1) >= T0
                if g + 1 < NG:
                    s1_load(g + 1)
                s2_uk(g)
                if g + 1 < NG:
                    s1_norm(g + 1)
                if need_q:
                    s2_q(g)
                s2_vf(g, need_q)
                if g + 1 < NG:
                    s1_tr(g + 1)
            P.barrier(); P.flush()

        with contextlib.ExitStack() as es5:
            C5 = s5_alloc(es5, sb)
            with contextlib.ExitStack() as esc:
                gen = s5_const_gen(nc, P, k, sb, pst, locals(), esc, C5)
                phase_fox(nc, P, k, sb, pst, locals(), bg=gen)
                for _ in gen:
                    pass
                P.barrier(); P.flush()
            s5_runtime(nc, P, k, sb, pst, locals(), C5)

        phase_outproj_mlp("0", YC_d, B_YC, N0, WOb[0], g_mixpost[0], xin[T0 * 128:NT * 128, :], Buf(), X1_d, B_X1,
                          X2_d, B_X2, W1b[0], W2b[0], g_mlppre[0], g_mlppost[0], B_WOb[0], B_W1b[0], B_W2b[0])

        phase_odd(nc, P, k, sb, pst, locals())
        phase_outproj_mlp("1", YC1_d, B_YC1, N1, WOb[1], g_mixpost[1], X2_d[128:N0 * 128, :], B_X2, X3_d, B_X3,
                          out, B_OUT, W1b[1], W2b[1], g_mlppre[1], g_mlppost[1], B_WOb[1], B_W1b[1], B_W2b[1])

        if dbg_o is not None:
            pass
        waits = []
        for p_, v_ in B_OUT.writers.items():
            P._need("sync", p_, v_, waits)
        P.cnt["sync"] += 1
        P.ops["sync"].append([waits, None, P.cnt["sync"], None])
        P.barrier(); P.flush()
    return nc


def trig_turns(k, X, n, tmpI, tmpA, tmpB, cosT, sinT, R, Bt, Bout):
    k.v("tensor_copy", tmpI, X, R=R + [Bt], W=[Bt])
    k.v("tensor_copy", tmpA, tmpI, R=[Bt], W=[Bt])
    k.v("tensor_tensor", X, X, tmpA, ALU.subtract, R=[Bt], W=[Bt])
    k.v("scalar_tensor_tensor", tmpA, X, 0.5, X, ALU.is_gt, ALU.subtract, R=[Bt], W=[Bt])
    k.v("scalar_tensor_tensor", X, tmpA, 0.5, tmpA, ALU.is_gt, ALU.subtract, R=[Bt], W=[Bt])
    k.act(sinT, X, AF.Sin, [Bt], [Bout], scale=TWO_PI)
    k.v("scalar_tensor_tensor", tmpB, X, -1.0, X, ALU.mult, ALU.max, R=[Bt], W=[Bt])
    k.act(cosT, tmpB, AF.Sin, [Bt], [Bout], scale=-TWO_PI, bias=math.pi / 2)


def trig_turns_fast(k, X, tmpI, tmpA, cosT, sinT, Bt, Bout, eng2="gpsimd"):
    k.v("tensor_copy", tmpI, X, R=[Bt], W=[Bt])
    k.v("tensor_copy", tmpA, tmpI, R=[Bt], W=[Bt], eng=eng2)
    k.v("tensor_tensor", X, X, tmpA, ALU.subtract, R=[Bt], W=[Bt], eng=eng2)
    k.act(sinT, X, AF.Sin, [Bt], [Bout], scale=TWO_PI)
    k.act(tmpA, X, AF.Abs, [Bt], [Bt])
    k.act(cosT, tmpA, AF.Sin, [Bt], [Bout], scale=-TWO_PI, bias=math.pi / 2)


class S5C:
    pass


def s5_alloc(es, sb):
    C = S5C()
    C.BsT = sb(es, "BsT", [128, 4, 2, 16, 128], BF16); C.BBsT = Buf()
    C.Klag = sb(es, "Klag", [128, 4, 16, 128], BF16); C.BKl = Buf()
    C.CrEX = [sb(es, f"CrEX{c}", [128, 16, 17, 32], BF16) for c in range(2)]; C.BCr = Buf()
    C.rho16 = sb(es, "rho16", [128, 16], F32); C.f16 = sb(es, "f16t", [128, 16], F32); C.Brho = Buf()
    C.kF = sb(es, "kF", [128, 256], F32)
    C.dcol = sb(es, "dcol", [128, 4], F32); C.Bd = Buf()
    return C


def s5_const_gen(nc, P, k, sb, pst, L, es, C):
    identf, identb, maskE, maskP, B_const = L["identf"], L["identb"], L["maskE"], L["maskP"], L["B_const"]
    lam_re, lam_im, log_dt = L["lam_re"][0], L["lam_im"][0], L["log_dt"][0]
    b_re, b_im, c_re, c_im = L["b_re"][0], L["b_im"][0], L["c_re"][0], L["c_im"][0]
    s5_d = L["s5_d"][0]
    BsT, BBsT, Klag, BKl, CrEX, BCr = C.BsT, C.BBsT, C.Klag, C.BKl, C.CrEX, C.BCr
    fl = lambda t3: t3[:].rearrange("p a b -> p (a b)")
    Bt = Buf("s5tmp")
    ops = []

    def v(*a, **kw):
        kw.setdefault("R", [Bt]); kw.setdefault("W", [Bt])
        ops.append(lambda: k.v(*a, **kw))

    def act(*a, **kw):
        ops.append(lambda: k.act(*a, **kw))

    def tr(*a):
        ops.append(lambda: k.tr(*a))

    def mm(*a, **kw):
        ops.append(lambda: k.mm(*a, **kw))

    def dma(*a, **kw):
        ops.append(lambda: k.dma(*a, **kw))

    pc = pst(es, "pc1", [128, 512]); Bpc = Buf()
    pcb = pc.bitcast(BF16); Bpcb = Bpc
    C.pcb, C.Bpcb = pcb, Bpcb
    src16 = sb(es, "src16", [16, 3, 128], F32)
    ld16 = sb(es, "ld16", [16, 2], F32)
    PL = sb(es, "PL", [128, 3, 16], F32)
    d4 = sb(es, "d4", [4, 128], F32)
    sc = sb(es, "sc", [128, 12, 16], F32)
    kI = sb(es, "kI", [128, 256], I32); kF = C.kF
    mag = sb(es, "mag", [128, 16, 17], F32); ang = sb(es, "ang", [128, 16, 17], F32)
    Pre = sb(es, "Pre", [128, 16, 17], F32); Pim = sb(es, "Pim", [128, 16, 17], F32)
    Qre = sb(es, "Qre", [128, 16, 16], F32); Qim = sb(es, "Qim", [128, 16, 16], F32); Qt = sb(es, "Qt", [128, 16, 16], F32)
    tI = sb(es, "tI", [128, 272], I32); tA = sb(es, "tA", [128, 272], F32); tB = sb(es, "tB", [128, 272], F32)
    dma("sync", src16[:, 0, :], lam_re.rearrange("(pr e) p -> pr (e p)", e=2), [], [Bt])
    dma("sync", src16[:, 1, :], lam_im.rearrange("(pr e) p -> pr (e p)", e=2), [], [Bt])
    dma("sync", ld16[:], log_dt.rearrange("(pr e) -> pr e", e=2), [], [Bt])
    dma("sync", d4[:], s5_d.rearrange("(t p) -> t p", p=128), [], [Bt])
    v("tensor_copy", src16[:, 2, :].rearrange("q (e p) -> q e p", e=2), bcast(ld16[:], 2, 64))
    tr(pc[:, 64:68], d4[:], identf[0:4, 0:4], [Bt, B_const], [Bpc])
    v("tensor_copy", C.dcol[:], pc[:, 64:68], R=[Bpc], W=[C.Bd])
    for i in range(3):
        tr(pc[:, i * 16:(i + 1) * 16], src16[:, i, :], identf[0:16, 0:16], [Bt, B_const], [Bpc])
    v("tensor_copy", PL[:], pc[:, 0:48].rearrange("p (a b) -> p a b", a=3), R=[Bpc], W=[Bt])
    lr, li = PL[:, 0, :], PL[:, 1, :]
    dt_, A_, TH_ = sc[:, 0, :], sc[:, 1, :], sc[:, 2, :]
    act(dt_, PL[:, 2, :], AF.Exp, [Bt], [Bt])
    v("tensor_tensor", A_, lr, dt_, ALU.mult)
    v("tensor_tensor", TH_, li, dt_, ALU.mult)
    v("tensor_scalar", TH_, TH_, 1.0 / TWO_PI, None, ALU.mult)
    v("iota", kI[:], [[1, 256]], base=0, channel_multiplier=0, eng="gpsimd", R=[], W=[Bt])
    v("tensor_copy", kF[:], kI[:], W=[Bt, C.Brho])
    kF17 = bcast(kF[:, 0:17], 1, 16)
    v("tensor_tensor", mag[:], bcast(A_, 2, 17), kF17, ALU.mult)
    act(mag[:], mag[:], AF.Exp, [Bt], [Bt])
    v("tensor_tensor", ang[:], bcast(TH_, 2, 17), kF17, ALU.mult)
    ops.append(lambda: trig_turns(k, fl(ang), 272, tI[:], tA[:], tB[:], fl(Pre), fl(Pim), [], Bt, Bt))
    v("tensor_tensor", Pre[:], Pre[:], mag[:], ALU.mult)
    v("tensor_tensor", Pim[:], Pim[:], mag[:], ALU.mult)
    v("tensor_copy", C.rho16[:], mag[:, :, 16], W=[C.Brho])
    f16b = sc[:, 4, :]
    v("tensor_scalar", C.f16[:], TH_, 16.0, None, ALU.mult, W=[C.Brho])
    v("tensor_copy", tI[:, 0:16], C.f16[:], R=[C.Brho])
    v("tensor_copy", f16b, tI[:, 0:16])
    v("tensor_tensor", C.f16[:], C.f16[:], f16b, ALU.subtract, R=[Bt, C.Brho], W=[C.Brho])
    nr, den, rden, qre, qim, t1, t2 = (sc[:, i, :] for i in range(5, 12))
    are, aim = Pre[:, :, 1], Pim[:, :, 1]
    v("tensor_scalar", nr, are, -1.0, None, ALU.add)
    v("tensor_tensor", den, lr, lr, ALU.mult)
    v("tensor_tensor", t1, li, li, ALU.mult)
    v("tensor_tensor", den, den, t1, ALU.add)
    v("reciprocal", rden, den)
    v("tensor_tensor", t1, nr, lr, ALU.mult)
    v("tensor_tensor", t2, aim, li, ALU.mult)
    v("tensor_tensor", t1, t1, t2, ALU.add)
    v("tensor_tensor", qre, t1, rden, ALU.mult)
    v("tensor_tensor", t1, aim, lr, ALU.mult)
    v("tensor_tensor", t2, nr, li, ALU.mult)
    v("tensor_tensor", t1, t1, t2, ALU.subtract)
    v("tensor_tensor", qim, t1, rden, ALU.mult)
    P16r, P16i = Pre[:, :, 0:16], Pim[:, :, 0:16]
    v("tensor_tensor", Qre[:], bcast(qre, 2, 16), P16r, ALU.mult)
    v("tensor_tensor", Qt[:], bcast(qim, 2, 16), P16i, ALU.mult)
    v("tensor_tensor", Qre[:], Qre[:], Qt[:], ALU.subtract)
    v("tensor_tensor", Qim[:], bcast(qre, 2, 16), P16i, ALU.mult)
    v("tensor_tensor", Qt[:], bcast(qim, 2, 16), P16r, ALU.mult)
    v("tensor_tensor", Qim[:], Qim[:], Qt[:], ALU.add)
    n_eager = len(ops)
    Bre = sb(es, "Bre", [128, 16, 16], F32); Bim = sb(es, "Bim", [128, 16, 16], F32)
    dma("sync", Bre[:], b_re.rearrange("(pr e) p h -> (e p) pr h", e=2), [], [Bt])
    dma("sync", Bim[:], b_im.rearrange("(pr e) p h -> (e p) pr h", e=2), [], [Bt])
    BsF = sb(es, "BsF", [128, 4, 16, 16], F32); BsG = sb(es, "BsG", [128, 4, 16, 16], F32)
    EX1 = [sb(es, f"EX1_{i}", [128, 16, 4, 2, 16], BF16) for i in range(2)]
    EXP = [sb(es, f"EXP{c}", [128, 16, 4, 32], BF16) for c in range(2)]
    Bbs = Buf(); Bex = [Buf(), Buf()]; Bexp = Buf()
    for c in range(2):
        v("memset", EXP[c][:], 0.0, R=[], W=[Bexp], eng="gpsimd")
    blocks = []
    for gt in range(4):
        g4 = slice(gt * 4, (gt + 1) * 4)
        Qr4, Qi4 = bcast(Qre[:, g4, :], 3, 16), bcast(Qim[:, g4, :], 3, 16)
        Br4, Bi4 = bcast(Bre[:, g4, :], 2, 16), bcast(Bim[:, g4, :], 2, 16)
        for c in range(2):
            bi = len(blocks)
            E1 = EX1[bi % 2]; BE = Bex[bi % 2]
            n0 = len(ops)
            if c == 0:
                v("tensor_tensor", BsF[:], Qr4, Br4, ALU.mult, R=[Bt], W=[Bbs])
                v("tensor_tensor", BsG[:], Qi4, Bi4, ALU.mult, R=[Bt], W=[Bbs], eng="gpsimd")
                v("tensor_tensor", BsF[:], BsF[:], BsG[:], ALU.subtract, R=[Bbs], W=[Bbs])
            else:
                v("tensor_tensor", BsF[:], Qr4, Bi4, ALU.mult, R=[Bt], W=[Bbs])
                v("tensor_tensor", BsG[:], Qi4, Br4, ALU.mult, R=[Bt], W=[Bbs], eng="gpsimd")
                v("tensor_tensor", BsF[:], BsF[:], BsG[:], ALU.add, R=[Bbs], W=[Bbs])
            for e_ in range(2):
                v("tensor_scalar", E1[:, :, :, e_, :], BsF[:].rearrange("p a b h -> p b a h"),
                  maskE[:, e_:e_ + 1], None, ALU.mult, R=[Bbs, B_const], W=[BE])
            for a_ in range(4):
                v("tensor_copy", EXP[c][:, gt * 4 + a_, a_, :], E1[:, 0, a_, :, :].rearrange("p e h -> p (e h)"),
                  R=[BE], W=[Bexp])
            n1 = len(ops)
            for k0 in range(0, 16, 8):
                for kk in range(8):
                    tr(pcb[:, kk * 128:(kk + 1) * 128], E1[:, k0 + kk, :, :, :].rearrange("p a e h -> p (a e h)"),
                       identb[:], [BE, B_const], [Bpcb])
                v("tensor_copy", BsT[:, gt, c, k0:k0 + 8, :], pcb[:].rearrange("p (a b) -> p a b", a=8),
                  R=[Bpcb], W=[BBsT])
            n2 = len(ops)
            blocks.append((ops[n0:n1], ops[n1:n2]))
            del ops[n0:]
    ops.extend(blocks[0][0])
    for bi in range(len(blocks)):
        if bi + 1 < len(blocks):
            ops.extend(blocks[bi + 1][0])
        ops.extend(blocks[bi][1])
    SRa = sb(es, "SRa", [128, 4, 128], F32); SRb = sb(es, "SRb", [128, 4, 128], F32)
    cr4 = c_re.rearrange("(gt g8) h p -> (g8 h) gt p", gt=4)
    ci4 = c_im.rearrange("(gt g8) h p -> (g8 h) gt p", gt=4)
    dma("sync", SRa[:, :, 0:64], cr4, [], [Bt]); dma("sync", SRa[:, :, 64:128], ci4, [], [Bt])
    dma("sync", SRb[:, :, 0:64], ci4, [], [Bt]); dma("sync", SRb[:, :, 64:128], cr4, [], [Bt])
    Ta = sb(es, "Ta", [128, 4, 128], F32); Tb = sb(es, "Tb", [128, 4, 128], F32)
    for gt in range(4):
        tr(pc[:, gt * 128:(gt + 1) * 128], SRa[:, gt, :], identf[:], [Bt, B_const], [Bpc])
    v("tensor_copy", Ta[:], pc[:].rearrange("p (a b) -> p a b", a=4), R=[Bpc])
    for gt in range(4):
        tr(pc[:, gt * 128:(gt + 1) * 128], SRb[:, gt, :], identf[:], [Bt, B_const], [Bpc])
    v("tensor_copy", Tb[:], pc[:].rearrange("p (a b) -> p a b", a=4), R=[Bpc])
    CTre = sb(es, "CTre", [128, 16, 16], F32); CTim = sb(es, "CTim", [128, 16, 16], F32)
    va = Ta[:].rearrange("p g (a e h) -> p (g a) e h", a=4, e=2)
    vb = Tb[:].rearrange("p g (a e h) -> p (g a) e h", a=4, e=2)
    m0, m1 = maskE[:, 0:1], maskE[:, 1:2]
    v("tensor_scalar", CTre[:], va[:, :, 0, :], m0, None, ALU.mult, R=[Bt, B_const])
    v("scalar_tensor_tensor", CTre[:], vb[:, :, 1, :], m1, CTre[:], ALU.mult, ALU.add, R=[Bt, B_const])
    v("tensor_scalar", CTim[:], vb[:, :, 0, :], m0, None, ALU.mult, R=[Bt, B_const])
    v("scalar_tensor_tensor", CTim[:], va[:, :, 1, :], m1, CTim[:], ALU.mult, ALU.add, R=[Bt, B_const])
    CrF = sb(es, "CrF", [128, 4, 17, 16], F32); CrG = sb(es, "CrG", [128, 4, 17, 16], F32)
    Bcf = Buf(); Bcg = [Buf() for _ in range(4)]
    cblocks = []
    for gt in range(4):
        g4 = slice(gt * 4, (gt + 1) * 4)
        Pr4, Pi4 = bcast(Pre[:, g4, :], 3, 16), bcast(Pim[:, g4, :], 3, 16)
        Cr4, Ci4 = bcast(CTre[:, g4, :], 2, 17), bcast(CTim[:, g4, :], 2, 17)
        n0 = len(ops)
        for c in range(2):
            if c == 0:
                v("tensor_tensor", CrF[:], Cr4, Pr4, ALU.mult, R=[Bt], W=[Bcf])
                v("tensor_tensor", CrG[:], Ci4, Pi4, ALU.mult, R=[Bt], W=[Bcf], eng="gpsimd")
                v("tensor_tensor", CrF[:], CrF[:], CrG[:], ALU.subtract, R=[Bcf], W=[Bcf])
            else:
                v("tensor_tensor", CrF[:], Cr4, Pi4, ALU.mult, R=[Bt], W=[Bcf])
                v("tensor_tensor", CrG[:], Ci4, Pr4, ALU.mult, R=[Bt], W=[Bcf], eng="gpsimd")
                v("scalar_tensor_tensor", CrF[:], CrF[:], -1.0, CrG[:], ALU.mult, ALU.subtract, R=[Bcf], W=[Bcf])
            src = CrF[:].rearrange("p a b h -> p (a b) h")
            v("tensor_tensor", CrEX[c][:, g4, :, :].rearrange("p a b (e h) -> p (a b) e h", e=2), bcast(src, 2, 2),
              bcast(bcast(maskE[:], 1, 68), 3, 16), ALU.mult, R=[Bcf, B_const], W=[Bcg[gt]])
        n1 = len(ops)
        for a_ in range(4):
            pr = gt * 4 + a_
            for c in range(2):
                mm(pc[:, :], EXP[c][:, pr, :, :].rearrange("p a h -> p (a h)"),
                   CrEX[c][:, pr, 0:16, :].rearrange("p a h -> p (a h)"),
                   (a_ == 0 and c == 0), (a_ == 3 and c == 1), [Bexp, Bcg[gt]], [Bpc])
        srcp = pc[:].rearrange("p (j h) -> p j h", j=16)
        v("tensor_tensor", Klag[:, gt, :, :].rearrange("p j (a h) -> p j a h", a=4), bcast(srcp, 2, 4),
          bcast(bcast(maskP[:], 1, 16), 3, 32), ALU.mult, R=[Bpc, B_const], W=[BKl])
        n2 = len(ops)
        cblocks.append((ops[n0:n1], ops[n1:n2]))
        del ops[n0:]
    ops.extend(cblocks[0][0])
    for bi in range(len(cblocks)):
        if bi + 1 < len(cblocks):
            ops.extend(cblocks[bi + 1][0])
        ops.extend(cblocks[bi][1])
    def run():
        for o in ops:
            o()
            yield
    g = run()
    for _ in range(n_eager):
        next(g)
    return g


def s5_runtime(nc, P, k, sb, pst, L, C):
    UT_d, B_UT, YC_d, B_YC = L["UT_d"], L["B_UT"], L["YC_d"], L["B_YC"]
    w_glu = L["w_glu"][0]
    BsT, BBsT, Klag, BKl, CrEX, BCr, rho16, Brho = C.BsT, C.BBsT, C.Klag, C.BKl, C.CrEX, C.BCr, C.rho16, C.Brho
    dcol, Bd = C.dcol, C.Bd
    with contextlib.ExitStack() as e1:
        Xs = sb(e1, "Xs", [128, 16, 2, 260], BF16); BXs = Buf()
        wg = sb(e1, "wg", [128, 4, 512], BF16); Bwg = Buf()
        for ft in range(4):
            k.dma("gpsimd", wg[:, ft, :], w_glu[ft * 128:(ft + 1) * 128, :], [], [Bwg])
        with contextlib.ExitStack() as e2:
            UTs = sb(e2, "UTs", [128, 4, 16, 256], BF16); BUT = Buf()
            tmpU = [sb(e2, f"tmpU{i}", [128, NT * 128], BF16) for i in range(2)]; BtU = [Buf(), Buf()]
            for ft in range(4):
                k.dma("sync", tmpU[ft % 2][:], UT_d[ft], [B_UT], [BtU[ft % 2]])
                srcv = tmpU[ft % 2][:].rearrange("p (m s) -> p s m", s=16)
                if ft % 2 == 0:
                    k.v("tensor_copy", UTs[:, ft, :, :], srcv, R=[BtU[ft % 2]], W=[BUT])
                else:
                    k.act(UTs[:, ft, :, :], srcv, AF.Copy, [BtU[ft % 2]], [BUT])
            NI = 3
            s1 = [pst(e2, f"s1p{i}", [128, 2, 256]) for i in range(NI)]; Bs1 = [Buf() for _ in range(NI)]
            tw = [sb(e2, f"tw{i}", [128, 6, 256], F32) for i in range(NI)]
            Btw = [Buf() for _ in range(NI)]; Btw2 = [Buf() for _ in range(NI)]; Btwz = [Buf() for _ in range(NI)]
            sev = [sb(e2, f"sev{i}", [128, 2, 256], F32) for i in range(NI)]; Bsev = [Buf() for _ in range(NI)]
            tg = [sb(e2, f"tg{i}", [128, 5, 256], F32) for i in range(NI)]; Btg = [Buf() for _ in range(NI)]
            tgI = [sb(e2, f"tgI{i}", [128, 256], I32) for i in range(NI)]
            k.v("memset", Xs[:, :, :, 0:1], 0.0, W=[BXs])

            def pair_ops(pr, i):
                gt, r0 = pr // 4, (pr % 4) * 32
                G_ = tg[i]; BG = Btg[i]
                BXa, BXb = Buf(), Buf()
                k.v("tensor_scalar", G_[:, 0, :], C.kF[:, 0:256], C.f16[:, pr:pr + 1], None, ALU.mult, R=[Brho], W=[BG]); yield
                k.v("tensor_copy", tgI[i][:], G_[:, 0, :], R=[BG], W=[BG]); yield
                k.v("tensor_copy", G_[:, 1, :], tgI[i][:], R=[BG], W=[BG], eng="gpsimd"); yield
                k.v("tensor_tensor", G_[:, 0, :], G_[:, 0, :], G_[:, 1, :], ALU.subtract, R=[BG], W=[BG], eng="gpsimd"); yield
                k.act(G_[:, 4, :], G_[:, 0, :], AF.Sin, [BG], [BG], scale=TWO_PI); yield
                k.act(G_[:, 1, :], G_[:, 0, :], AF.Abs, [BG], [BG]); yield
                k.act(G_[:, 3, :], G_[:, 1, :], AF.Sin, [BG], [BG], scale=-TWO_PI, bias=math.pi / 2); yield
                cs, sn = G_[:, 3, :], G_[:, 4, :]
                for c in range(2):
                    for s in range(16):
                        k.mm(s1[i][:, c, :], BsT[r0:r0 + 32, gt, c, 15 - s, :], UTs[r0:r0 + 32, gt, s, :],
                             s == 0, s == 15, [BBsT, BUT], [Bs1[i]], tile_position=(r0, 0))
                    yield
                T = tw[i]
                BTa, BTb, BTz = Btw[i], Btw2[i], Btwz[i]
                se = sev[i]; Bse = Bsev[i]
                k.act(se[:], s1[i][:], AF.Copy, [Bs1[i]], [Bse]); yield
                R0 = [Bse, BG]
                k.v("tensor_tensor", T[:, 0, :], se[:, 0, :], cs, ALU.mult, R=R0, W=[BTa])
                k.v("tensor_tensor", T[:, 2, :], se[:, 1, :], cs, ALU.mult, R=R0, W=[BTb], eng="gpsimd"); yield
                k.v("tensor_tensor", T[:, 1, :], se[:, 1, :], sn, ALU.mult, R=R0, W=[BTa])
                k.v("tensor_tensor", T[:, 3, :], se[:, 0, :], sn, ALU.mult, R=R0, W=[BTb], eng="gpsimd"); yield
                k.v("tensor_tensor", T[:, 0, :], T[:, 0, :], T[:, 1, :], ALU.add, R=[BTa], W=[BTa])
                k.v("tensor_tensor", T[:, 2, :], T[:, 2, :], T[:, 3, :], ALU.subtract, R=[BTb], W=[BTb], eng="gpsimd"); yield
                rb = rho16[:, pr:pr + 1].to_broadcast([128, 256])
                k.v("tensor_tensor_scan", T[:, 4, :], rb, T[:, 0, :], 0.0, ALU.mult, ALU.add, R=[BTa, Brho], W=[BTz]); yield
                k.v("tensor_tensor_scan", T[:, 5, :], rb, T[:, 2, :], 0.0, ALU.mult, ALU.add, R=[BTb, Brho], W=[BTz]); yield
                k.v("tensor_tensor", T[:, 0, :], T[:, 4, :], cs, ALU.mult, R=[BTz, BG], W=[BTa], eng="gpsimd")
                k.v("tensor_tensor", T[:, 2, :], T[:, 4, :], sn, ALU.mult, R=[BTz, BG], W=[BTb]); yield
                k.v("tensor_tensor", T[:, 1, :], T[:, 5, :], sn, ALU.mult, R=[BTz, BG], W=[BTa], eng="gpsimd")
                k.v("tensor_tensor", T[:, 3, :], T[:, 5, :], cs, ALU.mult, R=[BTz, BG], W=[BTb]); yield
                k.v("tensor_tensor", Xs[:, pr, 0, 1:257], T[:, 0, :], T[:, 1, :], ALU.subtract, R=[BTa], W=[BXa], eng="gpsimd")
                k.v("tensor_tensor", Xs[:, pr, 1, 1:257], T[:, 2, :], T[:, 3, :], ALU.add, R=[BTb], W=[BXb]); yield

            for q in range((16 + NI - 1) // NI):
                gens = [pair_ops(q * NI + j, j) for j in range(NI) if q * NI + j < 16]
                live = list(gens)
                while live:
                    for g_ in list(live):
                        try:
                            next(g_)
                        except StopIteration:
                            live.remove(g_)
            P.barrier(); P.flush()

        with contextlib.ExitStack() as e2:
            UT = sb(e2, "UT", [128, 4, NT * 128], BF16); BUT = Buf()
            for ft in range(4):
                k.dma("sync", UT[:, ft, :], UT_d[ft], [B_UT], [BUT])
            yp = [pst(e2, f"yp{i}", [128, 512]) for i in range(2)]; Byp = [Buf(), Buf()]
            gp_ = [pst(e2, f"gp{i}", [128, 512]) for i in range(2)]; Bgp = [Buf(), Buf()]
            CB = pst(e2, "CBp", [128, 2048]); BCBq = [Buf() for _ in range(4)]
            CBs = [sb(e2, f"CBs{i}", [32, 2048], BF16) for i in range(2)]; BCBs = [Buf(), Buf()]
            identb, B_const = L["identb"], L["B_const"]
            yf = [sb(e2, f"yf{i}", [128, 512], F32) for i in range(2)]; Byf = [Buf(), Buf()]
            YG = sb(e2, "YG", [128, 4, 512], BF16); BYG = [Buf() for _ in range(4)]
            SG = [sb(e2, f"SG{i}", [128, 512], BF16) for i in range(2)]; BSG = [Buf(), Buf()]
            YA = [sb(e2, f"YA{i}", [128, 4, 512], BF16) for i in range(2)]; BYA = [Buf(), Buf()]
            groups = [(T0, 1)] + [(t, 4) for t in range(T1, NT, 4)]
            n = 0
            for gi, (t0, nt) in enumerate(groups):
                N = nt * 128
                nb = N // 16
                tok0 = t0 * 128
                b0 = tok0 // 16
                for gt in range(4):
                    i = n % 2; n += 1
                    Y = yp[i]; BY = Byp[i]
                    Y3 = Y[:, 0:N].rearrange("p (m r) -> p m r", r=16)
                    U3 = UT[:, gt, tok0:tok0 + N].rearrange("p (m r) -> p m r", r=16)
                    ci = (n // 1) % 2
                    for q_ in range(4):
                        for a_ in range(4):
                            pr = gt * 4 + a_
                            for c in range(2):
                                k.mm(CB[0:nb, q_ * 512:(q_ + 1) * 512].rearrange("p (r x) -> p r x", r=4)[:, :, a_ * 32:(a_ + 1) * 32],
                                     Xs[:, pr, c, b0:b0 + nb],
                                     CrEX[c][:, pr, 4 * q_ + 1:4 * q_ + 5, :], c == 0, c == 1, [BXs, BCr], [BCBq[q_]])
                        if q_ % 2 == 0:
                            k.v("tensor_copy", CBs[ci][0:nb, q_ * 512:(q_ + 1) * 512], CB[0:nb, q_ * 512:(q_ + 1) * 512],
                                R=[BCBq[q_]], W=[BCBs[ci]])
                        else:
                            k.act(CBs[ci][0:nb, q_ * 512:(q_ + 1) * 512], CB[0:nb, q_ * 512:(q_ + 1) * 512], AF.Copy,
                                  [BCBq[q_]], [BCBs[ci]])
                    k.mm(Y[:, 0:N], Klag[:, gt, 0, :], UT[:, gt, tok0:tok0 + N], True, False, [BKl, BUT], [BY],
                         skip_group_check=True)
                    for j in range(1, 16):
                        k.mm(Y3[:, :, j:16], Klag[:, gt, j, :], U3[:, :, 0:16 - j], False, False, [BKl, BUT], [BY],
                             skip_group_check=True)
                    for r in range(16):
                        k.mm(Y3[:, :, r], CBs[ci][0:nb, r * 128:(r + 1) * 128], identb[0:nb, 0:nb], False, r == 15,
                             [BCBs[ci], B_const], [BY], skip_group_check=True)
                    k.v("scalar_tensor_tensor", yf[i][:, 0:N], UT[:, gt, tok0:tok0 + N], dcol[:, gt:gt + 1], Y[:, 0:N],
                        ALU.mult, ALU.add, R=[BUT, Bd, BY], W=[Byf[i]])
                    k.act(YG[:, gt, 0:N], yf[i][:, 0:N], AF.Gelu_apprx_tanh, [Byf[i]], [BYG[gt]])
                ai = gi % 2
                for fo in range(4):
                    i = n % 2; n += 1
                    for ft in range(4):
                        k.mm(gp_[i][:, 0:N], wg[:, ft, fo * 128:(fo + 1) * 128], YG[:, ft, 0:N], ft == 0, ft == 3,
                             [Bwg, BYG[ft]], [Bgp[i]])
                    k.act(SG[i][:, 0:N], gp_[i][:, 0:N], AF.Sigmoid, [Bgp[i]], [BSG[i]])
                    k.v("tensor_tensor", YA[ai][:, fo, 0:N], YG[:, fo, 0:N], SG[i][:, 0:N], ALU.mult,
                        R=[BYG[fo], BSG[i]], W=[BYA[ai]], eng="gpsimd")
                c0 = (t0 - T0) * 128
                k.dma("sync", YC_d[0:4, :, c0:c0 + N].rearrange("f p t -> p f t"), YA[ai][:, :, 0:N], [BYA[ai]], [B_YC])
            P.barrier(); P.flush()


def phase_fox(nc, P, k, sb, pst, L, bg=None):
    identb, maskb, Fkb, B_const, B_Fkb = L["identb"], L["maskb"], L["Fkb"], L["B_const"], L["B_Fkb"]
    KT_d, B_KT, QT_d, B_QT, V_d, B_V, YC_d, B_YC = (L[x] for x in ["KT_d", "B_KT", "QT_d", "B_QT", "V_d", "B_V", "YC_d", "B_YC"])
    NS = 3
    with contextlib.ExitStack() as es:
        Vh = [sb(es, f"Vh{i}", [128, NT, 68], BF16) for i in range(2)]; BVh = [Buf(), Buf()]
        KTa = [sb(es, f"KTa{i}", [67, NT * 128], BF16) for i in range(2)]; BKa = [Buf(), Buf()]
        QTa = [sb(es, f"QTa{i}", [67, N0 * 128], BF16) for i in range(2)]; BQa = [Buf(), Buf()]
        YB = sb(es, "YB", [128, N0, 512], BF16); BYB = Buf()
        Pt = [sb(es, f"Pt{i}", [128, 512], BF16) for i in range(NS)]; BPt = [Buf() for _ in range(NS)]
        dn = sb(es, "dn", [128, 8], F32); Bdn = Buf()
        S = [pst(es, f"Sps{i}", [128, 512]) for i in range(NS)]; BS = [Buf() for _ in range(NS)]
        O = [pst(es, f"Ops{i}", [128, 512]) for i in range(4)]; BO = [Buf() for _ in range(4)]
        for i in range(2):
            k.v("memset", KTa[i][64:67, :], 1.0, W=[BKa[i]])

        def load_head(h):
            hi = h % 2
            k.dma("sync", KTa[hi][0:64, :], KT_d[h], [B_KT], [BKa[hi]])
            k.dma("sync", QTa[hi][:, :], QT_d[h], [B_QT], [BQa[hi]])
            k.dma("sync", Vh[hi][:], V_d[:, :, h * 68:(h + 1) * 68].rearrange("t p c -> p t c"), [B_V], [BVh[hi]])

        load_head(0)
        L["convert_all_weights"]()
        conv_pend = L["conv_pend"]
        groups = [(T0, 1)] + [(t, 4) for t in range(T1, NT, 4)]
        its = [(h, g0, nq, kt) for h in range(8) for (g0, nq) in groups for kt in range(g0 + nq)]

        def issue_S(n):
            h, g0, nq, kt = its[n]
            hi = h % 2
            q0 = (g0 - T0) * 128
            i0 = max(0, kt - g0)
            si = n % NS
            diag = kt >= g0
            k.mm(S[si][:, i0 * 128:nq * 128], KTa[hi][:, kt * 128:(kt + 1) * 128],
                 QTa[hi][:, q0 + i0 * 128:q0 + nq * 128], True, not diag, [BKa[hi], BQa[hi]], [BS[si]])
            if diag:
                k.mm(S[si][:, i0 * 128:(i0 + 1) * 128], identb[:], maskb[:], False, True, [B_const], [BS[si]])

        LOOK = 2
        for n in range(min(LOOK, len(its))):
            issue_S(n)
        for n, (h, g0, nq, kt) in enumerate(its):
            if n + LOOK < len(its):
                h2 = its[n + LOOK][0]
                if h2 != its[n + LOOK - 1][0]:
                    pass
                issue_S(n + LOOK)
            if kt == 0 and g0 == T0 and h + 1 < 8:
                load_head(h + 1)
            if bg is not None:
                next(bg, None)
            if conv_pend and n % 40 == 12:
                conv_pend.pop(0)()
            i0 = max(0, kt - g0)
            si = n % NS
            k.act(Pt[si][:, i0 * 128:nq * 128], S[si][:, i0 * 128:nq * 128], AF.Exp, [BS[si], B_Fkb], [BPt[si]],
                  scale=0.125, bias=Fkb[:, kt, h:h + 1])
            for i in range(i0, nq):
                last = (kt == g0 + i)
                k.mm(O[i][:, 0:65], Pt[si][:, i * 128:(i + 1) * 128], Vh[h % 2][:, kt, 0:65],
                     kt == 0, last, [BPt[si], BVh[h % 2]], [BO[i]])
                if last:
                    k.v("tensor_scalar", dn[:, i:i + 1], O[i][:, 64:65], 1e-30, None, ALU.max, R=[BO[i]], W=[Bdn])
                    k.v("reciprocal", dn[:, 4 + i:5 + i], dn[:, i:i + 1], R=[Bdn], W=[Bdn])
                    k.v("tensor_scalar", YB[:, g0 - T0 + i, h * 64:(h + 1) * 64], O[i][:, 0:64],
                        dn[:, 4 + i:5 + i], None, ALU.mult, R=[BO[i], Bdn], W=[BYB])
        while conv_pend:
            conv_pend.pop(0)()
        if bg is not None:
            for _ in bg:
                pass
        tp, Btp = L["C5"].pcb, L["C5"].Bpcb
        yt = [sb(es, f"ytF{i}", [128, 4, 128], BF16) for i in range(2)]; Byt = [Buf(), Buf()]
        for t in range(N0):
            i = t % 2
            for ft in range(4):
                k.tr(tp[:, ft * 128:(ft + 1) * 128], YB[:, t, ft * 128:(ft + 1) * 128], identb[:], [BYB, B_const], [Btp])
            k.v("tensor_copy", yt[i][:], tp[:, 0:512].rearrange("p (f t) -> p f t", f=4), R=[Btp], W=[Byt[i]])
            k.dma("gpsimd", YC_d[4:8, :, t * 128:(t + 1) * 128].rearrange("f p t -> p f t"), yt[i][:], [Byt[i]], [B_YC])
        P.barrier(); P.flush()


def phase_odd(nc, P, k, sb, pst, L):
    identb, tril, B_const = L["identb"], L["tril"], L["B_const"]
    X2_d, B_X2, YC1_d, B_YC1 = L["X2_d"], L["B_X2"], L["YC1_d"], L["B_YC1"]
    norm_in, to_fm, load_w_bf16, load_bcast_row, rms_rstd = (L[x] for x in
        ["norm_in", "to_fm", "load_w_bf16", "load_bcast_row", "rms_rstd"])
    WO = L["w_in_o"][0]
    pool_w, pool_scale, ln_g, ln_b, w_s, b_s = (L[x][0] for x in ["pool_w", "pool_scale", "ln_g", "ln_b", "w_s", "b_s"])
    g_row = L["g_mixpre"][1]
    icnt = L["icnt"]
    NTOK = N0 * 128
    with contextlib.ExitStack() as es:
        wi = sb(es, "wiO", [128, 8, 1536], BF16); Bwi = Buf()
        gp = sb(es, "gO", [128, D], F32); Bg = Buf()
        lg = sb(es, "lgO", [128, 512], F32); lb = sb(es, "lbO", [128, 512], F32)
        pw = sb(es, "pwO", [128, 4, 128], BF16); psc = sb(es, "pscO", [128, 4], F32)
        bs_ = sb(es, "bsO", [128, 4], F32); ic = sb(es, "icO", [128, 4, 16], F32)
        WsT = sb(es, "WsT", [128, 4, 128], BF16); BWs = Buf()
        load_bcast_row(gp, Bg, g_row, D)
        load_bcast_row(lg, Bg, ln_g, 512)
        load_bcast_row(lb, Bg, ln_b, 512)
        load_w_bf16(wi, Bwi, L["WIOb"], 8, 1536, srcbuf=L["B_WIOb"])
        k.dma("gpsimd", pw[:], pool_w.rearrange("g c d -> c g d"), [], [Bg])
        k.dma("sync", ic[:], icnt, [], [Bg])
        XC = sb(es, "XC", [128, 4, NTOK], F32); BXC = Buf()
        tp = pst(es, "tpO", [128, 1024], BF16); Btp = Buf()
        pp = [pst(es, f"ppO{i}", [128, 512]) for i in range(4)]; Bpp = [Buf() for _ in range(4)]
        with contextlib.ExitStack() as e2:
            r4 = sb(e2, "r4", [4, 2, 128], F32); Br4 = Buf()
            k.dma("sync", r4[:, 0, :], pool_scale.rearrange("(g p) -> g p", p=128), [], [Br4])
            k.dma("sync", r4[:, 1, :], b_s, [], [Br4])
            k.tr(pp[0][:, 0:4], r4[:, 0, :], L["identf"][0:4, 0:4], [Br4, B_const], [Bpp[0]])
            k.tr(pp[0][:, 4:8], r4[:, 1, :], L["identf"][0:4, 0:4], [Br4, B_const], [Bpp[0]])
            k.v("tensor_copy", psc[:], pp[0][:, 0:4], R=[Bpp[0]], W=[Bg])
            k.v("tensor_copy", bs_[:], pp[0][:, 4:8], R=[Bpp[0]], W=[Bg])
            wsf = sb(e2, "wsf", [128, 4, 128], F32); wsb = sb(e2, "wsb", [128, 4, 128], BF16); Bt = Buf()
            k.dma("sync", wsf[:], w_s.rearrange("g t s -> t g s"), [], [Bt])
            k.v("tensor_tensor", wsb[:], wsf[:], bcast(tril[:], 1, 4), ALU.mult, R=[Bt, B_const], W=[Bt])
            for g in range(4):
                k.tr(tp[:, g * 128:(g + 1) * 128], wsb[:, g, :], identb[:], [Bt, B_const], [Btp])
            k.v("tensor_copy", WsT[:], tp[:, 0:512].rearrange("p (g t) -> p g t", g=4), R=[Btp], W=[BWs])
            P.barrier(); P.flush()
        GT = 4
        xt = [sb(es, f"xO{i}", [128, D], F32) for i in range(4)]; Bxt = [Buf() for _ in range(4)]
        hb = [sb(es, f"hbO{i}", [128, D], BF16) for i in range(2)]; Bhb = [Buf(), Buf()]
        jk = sb(es, "jO", [128, D], BF16); Bj = Buf()
        st = [sb(es, f"sO{i}", [128, 4], F32) for i in range(2)]; Bst = [Buf(), Buf()]
        hT = [sb(es, f"hTO{i}", [128, 8, GT * 128], BF16) for i in range(2)]; BhT = [Buf(), Buf()]
        NU = 3
        ug = [sb(es, f"ugO{i}", [128, 512], BF16) for i in range(NU)]; Bug = [Buf() for _ in range(NU)]
        vg = [sb(es, f"vgO{i}", [128, 512], F32) for i in range(2)]; Bvg = [Buf(), Buf()]
        vn = [sb(es, f"vnO{i}", [128, 512], BF16) for i in range(2)]; Bvn = [Buf(), Buf()]
        bst = [sb(es, f"bstO{i}", [128, 8], F32) for i in range(2)]; Bbst = [Buf(), Buf()]
        yd = [sb(es, f"ydO{i}", [128, 512], BF16) for i in range(2)]; Byd = [Buf(), Buf()]
        yt = [sb(es, f"ytO{i}", [128, 4, 128], BF16) for i in range(2)]; Byt = [Buf(), Buf()]
        tp2, Btp2 = pp[1].bitcast(BF16), Bpp[1]
        pm1 = pst(es, "pmO0", [128, 512]); Bpm1 = Buf()
        pm_ = [pm1, pm1]; Bpm = [Bpm1, Bpm1]
        puvb = [pst(es, f"puvO{i}", [128, 512]) for i in range(2)]; Bpuvb = [Buf(), Buf()]
        ctr = {"pp": 0, "x": 0}

        def nextpp():
            return pp[0], Bpp[0]

        tile_groups = [(0, 1)] + [(1 + 4 * i, 4) for i in range(4)]

        def g0(G):
            l0, nt = tile_groups[G]
            N = nt * 128
            H, BH = hT[G % 2], BhT[G % 2]
            for j in range(nt):
                lt = l0 + j
                xi = ctr["x"] % 4; ctr["x"] += 1
                k.dma("sync", xt[xi][:], X2_d[lt * 128:(lt + 1) * 128, :], [B_X2], [Bxt[xi]])
                norm_in(xt[xi], Bxt[xi], gp, Bg, hb[j % 2], Bhb[j % 2], jk, Bj, st[j % 2], Bst[j % 2])
                to_fm(hb[j % 2], Bhb[j % 2], tp, Btp, H, BH, j * 128, j % 2 == 1)
            for ft in range(4):
                p_, Bp_ = nextpp()
                for kt in range(8):
                    k.mm(p_[:, 0:N], wi[:, kt, ft * 128:(ft + 1) * 128], H[:, kt, 0:N], kt == 0, kt == 7, [Bwi, BH], [Bp_])
                if ft % 2 == 0:
                    k.v("tensor_copy", XC[:, ft, l0 * 128:l0 * 128 + N], p_[:, 0:N], R=[Bp_], W=[BXC])
                else:
                    k.act(XC[:, ft, l0 * 128:l0 * 128 + N], p_[:, 0:N], AF.Copy, [Bp_], [BXC])

        def tinfo(o):
            return 1 + o // 4, o % 4

        NV = 5
        vgq = [sb(es, f"vgq{i}", [128, 512], F32) for i in range(NV)]; Bvgq = [Buf() for _ in range(NV)]
        vnn = [sb(es, f"vnn{i}", [128, 512], F32) for i in range(2)]; Bvnn = [Buf(), Buf()]
        bsq = [sb(es, f"bsq{i}", [128, 8], F32) for i in range(4)]; Bbsq = [Buf() for _ in range(4)]
        NUU = 8
        ugq = [sb(es, f"ugq{i}", [128, 512], BF16) for i in range(NUU)]; Bugq = [Buf() for _ in range(NUU)]
        puv = {}

        def stA(o):
            G, j = tinfo(o)
            H, BH = hT[G % 2], BhT[G % 2]
            if o % 2 == 0:
                pu, Bpu, pv, Bpv = pp[2], Bpp[2], pp[3], Bpp[3]
            else:
                pu, Bpu, pv, Bpv = puvb[0], Bpuvb[0], puvb[1], Bpuvb[1]
            for kt in range(8):
                k.mm(pu[:, :], H[:, kt, j * 128:(j + 1) * 128], wi[:, kt, 512:1024], kt == 0, kt == 7, [Bwi, BH], [Bpu])
            for kt in range(8):
                k.mm(pv[:, :], H[:, kt, j * 128:(j + 1) * 128], wi[:, kt, 1024:1536], kt == 0, kt == 7, [Bwi, BH], [Bpv])
            puv[o] = (pu, Bpu, pv, Bpv)

        def stB(o):
            pu, Bpu, pv, Bpv = puv.pop(o)
            k.act(ugq[o % NUU][:], pu[:, :], AF.Gelu_apprx_tanh, [Bpu], [Bugq[o % NUU]])
            k.act(vgq[o % NV][:], pv[:, :], AF.Gelu_apprx_tanh, [Bpv], [Bvgq[o % NV]])

        def stC(o):
            S_, BS_ = bsq[o % 4], Bbsq[o % 4]
            k.v("bn_stats", S_[:, 0:6], vgq[o % NV][:], R=[Bvgq[o % NV]], W=[BS_])
            k.v("bn_aggr", S_[:, 6:8], S_[:, 0:6], R=[BS_], W=[BS_])
            k.v("tensor_scalar", S_[:, 0:1], S_[:, 7:8], EPS, None, ALU.add, R=[BS_], W=[BS_])

        def stD(o):
            S_, BS_ = bsq[o % 4], Bbsq[o % 4]
            k.act(S_[:, 0:1], S_[:, 0:1], AF.Sqrt, [BS_], [BS_])

        def stE(o):
            S_, BS_ = bsq[o % 4], Bbsq[o % 4]
            k.v("reciprocal", S_[:, 1:2], S_[:, 0:1], R=[BS_], W=[BS_])
            k.v("tensor_scalar", vnn[o % 2][:], vgq[o % NV][:], S_[:, 6:7], S_[:, 1:2], ALU.subtract, ALU.mult,
                R=[Bvgq[o % NV], BS_], W=[Bvnn[o % 2]])

        def stF(o):
            k.v("tensor_tensor", vnn[o % 2][:], vnn[o % 2][:], lg[:], ALU.mult, R=[Bvnn[o % 2], Bg], W=[Bvnn[o % 2]], eng="gpsimd")
            k.v("tensor_tensor", vn[o % 2][:], vnn[o % 2][:], lb[:], ALU.add, R=[Bvnn[o % 2], Bg], W=[Bvn[o % 2]], eng="gpsimd")

        def stG(o):
            pm, Bp = pm_[o % 2], Bpm[o % 2]
            for g in range(4):
                k.mm(pm[:, g * 128:(g + 1) * 128], WsT[:, g, :], vn[o % 2][:, g * 128:(g + 1) * 128], True, True,
                     [BWs, Bvn[o % 2]], [Bp])

        def stH(o):
            pm, Bp = pm_[o % 2], Bpm[o % 2]
            for g in range(4):
                k.v("scalar_tensor_tensor", yd[o % 2][:, g * 128:(g + 1) * 128], pm[:, g * 128:(g + 1) * 128], bs_[:, g:g + 1],
                    ugq[o % NUU][:, g * 128:(g + 1) * 128], ALU.add, ALU.mult, R=[Bp, Bg, Bugq[o % NUU]], W=[Byd[o % 2]])

        def stI(o):
            for ft in range(4):
                k.tr(tp2[:, ft * 128:(ft + 1) * 128], yd[o % 2][:, ft * 128:(ft + 1) * 128], identb[:], [Byd[o % 2], B_const], [Btp2])

        def stJ(o):
            k.act(yt[o % 2][:], tp2[:, 0:512].rearrange("p (f t) -> p f t", f=4), AF.Copy, [Btp2], [Byt[o % 2]])
            k.dma("sync", YC1_d[4:8, :, o * 128:(o + 1) * 128].rearrange("f p t -> p f t"), yt[o % 2][:], [Byt[o % 2]], [B_YC1])

        stages = [stA, stB, stC, stD, stE, stF, stG, stH, stI, stJ]

        SWW = 512 + 48
        SW = [sb(es, f"SW{i}", [128, 2, SWW], F32) for i in range(2)]; BSW = [Buf(), Buf()]
        for i in range(2):
            k.v("memset", SW[i][:], 0.0, W=[BSW[i]], eng="gpsimd")
        PLd = [sb(es, f"PLd{i}", [128, 4, 512], BF16) for i in range(2)]; BPL = [Buf(), Buf()]
        fix = sb(es, "fix", [128, 16], F32); Bfix = Buf()
        yc = [sb(es, f"ycO{i}", [128, 512], BF16) for i in range(2)]; Byc = [Buf(), Buf()]
        pctr = [0]

        def pool_group(G):
            a_ = tile_groups[G][0] * 128
            b_ = a_ + 512
            o0 = a_ - 128
            lo = a_ - 15
            pb = G % 2
            for g in range(4):
                w = 2 << g
                x = XC[:, g, :]
                eng = "vector" if (g + G) % 2 == 0 else "gpsimd"
                cur, coff = x, 0
                for s_i in range(g + 1):
                    d = 1 << s_i
                    dst = SW[g % 2][:, s_i % 2, :]
                    k.v("tensor_tensor", dst[:, 16:16 + b_ - lo], cur[:, lo - coff:b_ - coff], cur[:, lo - d - coff:b_ - d - coff],
                        ALU.add, R=[BXC, BSW[g % 2]], W=[BSW[g % 2]], eng=eng)
                    cur, coff = dst, lo - 16
                k.v("scalar_tensor_tensor", PLd[pb][:, g, :], cur[:, a_ - coff:b_ - coff], 1.0 / w, x[:, a_:b_], ALU.mult,
                    ALU.subtract, R=[BSW[g % 2], BXC], W=[BPL[pb]])
                if G == 1:
                    k.v("tensor_tensor", fix[:], cur[:, a_ - coff:a_ - coff + 16], ic[:, g, :], ALU.mult,
                        R=[BSW[g % 2], Bg], W=[Bfix])
                    k.v("tensor_tensor", PLd[pb][:, g, 0:16], fix[:], x[:, a_:a_ + 16], ALU.subtract, R=[Bfix, BXC], W=[BPL[pb]])
            for g in range(4):
                p_, Bp_ = nextpp()
                k.mm(p_[:, :], pw[:, g, :], PLd[pb][:, g, :], True, True, [Bg, BPL[pb]], [Bp_])
                i = pctr[0] % 2; pctr[0] += 1
                k.v("tensor_scalar", yc[i][:], p_[:, :], psc[:, g:g + 1], None, ALU.mult, R=[Bp_, Bg], W=[Byc[i]])
                k.dma("gpsimd", YC1_d[g, :, o0:o0 + 512], yc[i][:], [Byc[i]], [B_YC1])

        g0(0); g0(1)
        NS_ = len(stages)
        bgq = []
        for i in range(N1 + NS_ - 1):
            if i % 4 == 1 and (i // 4 + 2) < len(tile_groups):
                G2 = i // 4 + 2
                bgq.extend(k.record(lambda: g0(G2)))
                deadline = i + 2
            per = max(4, (len(bgq) + 19) // 20)
            for s_ in range(NS_ - 1, -1, -1):
                o = i - s_
                if 0 <= o < N1:
                    stages[s_](o)
                for _ in range(per):
                    if bgq:
                        bgq.pop(0)()
            if i % 4 == 3:
                while bgq:
                    bgq.pop(0)()
            if i % 4 == 3 and 1 + i // 4 < len(tile_groups):
                G1 = 1 + i // 4
                bgq.extend(k.record(lambda: pool_group(G1)))
        while bgq:
            bgq.pop(0)()
        P.barrier(); P.flush()


_W_NAMES = ["mix_pre_g", "mix_post_g", "mlp_pre_g", "mlp_post_g", "w_in_even", "s5_lam_re", "s5_lam_im", "s5_log_dt",
            "s5_b_re", "s5_b_im", "s5_c_re", "s5_c_im", "s5_d", "s5_w_glu", "fox_b_f", "w_out_even", "w_in_odd",
            "pool_w", "pool_scale", "sgu_ln_g", "sgu_ln_b", "sgu_w_s", "sgu_b_s", "w_out_odd", "mlp_w1", "mlp_w2"]


def make_in_maps(inputs):
    x = np.ascontiguousarray(np.asarray(inputs["x"], dtype=np.float32))
    shared = {n: np.ascontiguousarray(np.asarray(inputs[n], dtype=np.float32)) for n in _W_NAMES}
    maps = []
    for c in range(8):
        b, r = c // 2, c % 2
        xin = np.zeros((NT * 128, D), np.float32)
        kb = np.zeros((128, NT), np.float32)
        ic = np.zeros((128, 4, 16), np.float32)
        if r == 0:
            xin[2048:] = x[b, :2048]
            kb[:, :16] = -30000.0
            for g in range(4):
                w = 2 << g
                ic[:, g, :] = 1.0 / np.minimum(np.arange(16) + 1.0, float(w))[None, :]
        else:
            xin[:] = x[b]
            for g in range(4):
                ic[:, g, :] = 1.0 / float(2 << g)
        m = {"xin": xin, "kbias": kb, "icnt": ic}
        m.update(shared)
        maps.append(m)
    return maps


_NC_CACHE = {}


def kernel(**inputs):
    if "nc" not in _NC_CACHE:
        _NC_CACHE["nc"] = build_program()
    nc = _NC_CACHE["nc"]
    maps = make_in_maps(inputs)
    res = run_bass_kernel_spmd(nc, maps, core_ids=list(range(8)))
    outp = np.zeros((4, 4096, D), np.float32)
    for c in range(8):
        b, r = c // 2, c % 2
        outp[b, r * 2048:(r + 1) * 2048] = res.results[c]["out"]
    return outp
```
